# Optimizing a Trainium2 kernel written in Bass

```python
import jax, jax.numpy as jnp
from jax import lax
import numpy as np

D_MODEL = 1024
BATCH = 4
SEQ = 4096
DEPTH = 1

D_MIX = 2 * D_MODEL
D_SSD = D_MIX // 2
SSD_HEAD_DIM = 64
SSD_HEADS = D_SSD // SSD_HEAD_DIM
SSD_GROUPS = 4
SSD_STATE = 128
SSD_CONV = 4
SSD_CHUNK = 128
D_POOL = D_MIX - D_SSD
POOL_WINDOWS = (2, 4, 8, 16)
POOL_GROUPS = len(POOL_WINDOWS)
POOL_GROUP_DIM = D_POOL // POOL_GROUPS
D_XBC = D_SSD + 2 * SSD_GROUPS * SSD_STATE
D_IN_PROJ = D_SSD + D_XBC + SSD_HEADS + D_POOL
D_FF = 2816
N_MOD = 9
FFN_RES = 0.5
EPS = 1e-6

kernel_name = "hybrid_ssd_pool_macaron_adaln"


def rms_norm(x, w):
    xf = x.astype(jnp.float32)
    y = xf * lax.rsqrt(jnp.mean(xf * xf, axis=-1, keepdims=True) + EPS)
    return (y * w.astype(jnp.float32)).astype(x.dtype)


def modulate(h, shift, scale):
    return h * (1 + scale[:, None, :]) + shift[:, None, :]


def swiglu(h, w_gate, w_up, w_down):
    return (jax.nn.silu(h @ w_gate) * (h @ w_up)) @ w_down


def causal_depthwise_conv(u, w, b):
    ch = u.shape[-1]
    y = lax.conv_general_dilated(
        u, w[:, None, :].astype(u.dtype), window_strides=(1,),
        padding=[(SSD_CONV - 1, 0)], dimension_numbers=('NWC', 'WIO', 'NWC'),
        feature_group_count=ch)
    return y + b.astype(u.dtype)


def ssd_chunked(xh, dt, a, bm, cm):
    bsz, L, H, P = xh.shape
    G, N = bm.shape[2], bm.shape[3]
    R = H // G
    Q = SSD_CHUNK
    NC = L // Q
    xdt = (xh * dt[..., None]).reshape(bsz, NC, Q, G, R, P)
    adt = jnp.transpose((dt * a).reshape(bsz, NC, Q, G, R), (0, 3, 4, 1, 2))
    a_cs = jnp.cumsum(adt, axis=-1)
    bc = bm.reshape(bsz, NC, Q, G, N)
    cc = cm.reshape(bsz, NC, Q, G, N)
    causal = jnp.tril(jnp.ones((Q, Q), dtype=bool))
    seg = a_cs[..., :, None] - a_cs[..., None, :]
    decay = jnp.exp(jnp.where(causal, seg, -jnp.inf))
    cb = jnp.einsum('bclgn,bcsgn->bgcls', cc, bc)
    scores = cb[:, :, None] * decay
    y_diag = jnp.einsum('bgrcls,bcsgrp->bclgrp', scores, xdt)
    ds = jnp.transpose(jnp.exp(a_cs[..., -1:] - a_cs), (0, 3, 4, 1, 2))
    states = jnp.einsum('bcsgn,bcsgrp->bcgrpn', bc, xdt * ds[..., None])
    chunk_decay = jnp.moveaxis(jnp.exp(a_cs[..., -1]), -1, 0)

    def step(h, inp):
        s, d = inp
        return h * d[..., None, None] + s, h

    h0 = jnp.zeros((bsz, G, R, P, N), states.dtype)
    _, prev = lax.scan(step, h0, (jnp.moveaxis(states, 1, 0), chunk_decay))
    prev = jnp.moveaxis(prev, 0, 1)
    sd = jnp.transpose(jnp.exp(a_cs), (0, 3, 4, 1, 2))
    y_off = jnp.einsum('bclgn,bcgrpn->bclgrp', cc, prev) * sd[..., None]
    return (y_diag + y_off).reshape(bsz, L, H, P)


def pool_mixer(u, pool_w, pool_b, pool_scale):
    bsz, L, _ = u.shape
    uf = u.astype(jnp.float32).reshape(bsz, L, POOL_GROUPS, POOL_GROUP_DIM)
    cs = jnp.cumsum(uf, axis=1)
    pos = jnp.arange(1, L + 1, dtype=jnp.float32)
    pooled = []
    for gi, w in enumerate(POOL_WINDOWS):
        shifted = jnp.pad(cs[:, :L - w, gi], ((0, 0), (w, 0), (0, 0)))
        cnt = jnp.minimum(pos, float(w))[None, :, None]
        pooled.append((cs[:, :, gi] - shifted) / cnt)
    diff = jnp.stack(pooled, axis=2) - uf
    out = jnp.einsum('blgc,gcd->blgd', diff, pool_w.astype(jnp.float32)) + pool_b.astype(jnp.float32)
    out = out.reshape(bsz, L, D_POOL) * pool_scale.astype(jnp.float32)
    return out.astype(u.dtype)


def token_mixer(h, w_in, conv_w, conv_b, dt_bias, a_log, d_skip, ssd_norm_w,
                pool_w, pool_b, pool_scale, w_out):
    bsz, L, _ = h.shape
    f32 = jnp.float32
    proj = h @ w_in
    z, xbc, dt_raw, u = jnp.split(
        proj, [D_SSD, D_SSD + D_XBC, D_SSD + D_XBC + SSD_HEADS], axis=-1)
    xbc = jax.nn.silu(causal_depthwise_conv(xbc, conv_w, conv_b))
    xs, bm, cm = jnp.split(xbc, [D_SSD, D_SSD + SSD_GROUPS * SSD_STATE], axis=-1)
    dt = jax.nn.softplus(dt_raw.astype(f32) + dt_bias.astype(f32))
    a = -jnp.exp(a_log.astype(f32))
    xh = xs.astype(f32).reshape(bsz, L, SSD_HEADS, SSD_HEAD_DIM)
    y = ssd_chunked(xh, dt, a,
                    bm.astype(f32).reshape(bsz, L, SSD_GROUPS, SSD_STATE),
                    cm.astype(f32).reshape(bsz, L, SSD_GROUPS, SSD_STATE))
    y = (y + d_skip.astype(f32)[:, None] * xh).reshape(bsz, L, D_SSD)
    yg = (y * jax.nn.silu(z.astype(f32))).reshape(bsz, L, SSD_GROUPS, D_SSD // SSD_GROUPS)
    yg = yg * lax.rsqrt(jnp.mean(yg * yg, axis=-1, keepdims=True) + EPS)
    y_ssd = (yg.reshape(bsz, L, D_SSD) * ssd_norm_w.astype(f32)).astype(h.dtype)
    y_pool = pool_mixer(u, pool_w, pool_b, pool_scale)
    return jnp.concatenate([y_ssd, y_pool], axis=-1) @ w_out


def setup_inputs(seed: int = 0) -> dict:
    key = jax.random.key(seed)
    ks = jax.random.split(key, 32)
    f32 = jnp.float32

    def normal(k, shape, scale):
        return jax.random.normal(k, shape, f32) * scale

    dt0 = jnp.exp(jax.random.uniform(ks[9], (DEPTH, SSD_HEADS), f32,
                                     minval=np.log(1e-3), maxval=np.log(1e-1)))
    return {
        "x": normal(ks[0], (BATCH, SEQ, D_MODEL), 1.0),
        "c": normal(ks[1], (BATCH, D_MODEL), 1.0),
        "w_ada": normal(ks[2], (DEPTH, D_MODEL, N_MOD * D_MODEL), 0.5 * D_MODEL ** -0.5),
        "b_ada": normal(ks[3], (DEPTH, N_MOD * D_MODEL), 0.02),
        "ffn1_norm": 1.0 + normal(ks[4], (DEPTH, D_MODEL), 0.02),
        "ffn1_w_gate": normal(ks[5], (DEPTH, D_MODEL, D_FF), D_MODEL ** -0.5),
        "ffn1_w_up": normal(ks[6], (DEPTH, D_MODEL, D_FF), D_MODEL ** -0.5),
        "ffn1_w_down": normal(ks[7], (DEPTH, D_FF, D_MODEL), D_FF ** -0.5),
        "mix_norm": 1.0 + normal(ks[8], (DEPTH, D_MODEL), 0.02),
        "w_in": normal(ks[10], (DEPTH, D_MODEL, D_IN_PROJ), D_MODEL ** -0.5),
        "conv_w": normal(ks[11], (DEPTH, SSD_CONV, D_XBC), SSD_CONV ** -0.5),
        "conv_b": normal(ks[12], (DEPTH, D_XBC), 0.02),
        "dt_bias": dt0 + jnp.log(-jnp.expm1(-dt0)),
        "a_log": jnp.log(jax.random.uniform(ks[13], (DEPTH, SSD_HEADS), f32, minval=1.0, maxval=16.0)),
        "d_skip": 1.0 + normal(ks[14], (DEPTH, SSD_HEADS), 0.02),
        "ssd_norm_w": 1.0 + normal(ks[15], (DEPTH, D_SSD), 0.02),
        "pool_w": normal(ks[16], (DEPTH, POOL_GROUPS, POOL_GROUP_DIM, POOL_GROUP_DIM), POOL_GROUP_DIM ** -0.5),
        "pool_b": normal(ks[17], (DEPTH, POOL_GROUPS, POOL_GROUP_DIM), 0.02),
        "pool_scale": 1.0 + normal(ks[18], (DEPTH, D_POOL), 0.02),
        "w_out": normal(ks[19], (DEPTH, D_MIX, D_MODEL), D_MIX ** -0.5),
        "ffn2_norm": 1.0 + normal(ks[20], (DEPTH, D_MODEL), 0.02),
        "ffn2_w_gate": normal(ks[21], (DEPTH, D_MODEL, D_FF), D_MODEL ** -0.5),
        "ffn2_w_up": normal(ks[22], (DEPTH, D_MODEL, D_FF), D_MODEL ** -0.5),
        "ffn2_w_down": normal(ks[23], (DEPTH, D_FF, D_MODEL), D_FF ** -0.5),
        "final_norm": 1.0 + normal(ks[24], (D_MODEL,), 0.02),
    }


def reference(x, c, w_ada, b_ada, ffn1_norm, ffn1_w_gate, ffn1_w_up, ffn1_w_down,
              mix_norm, w_in, conv_w, conv_b, dt_bias, a_log, d_skip, ssd_norm_w,
              pool_w, pool_b, pool_scale, w_out, ffn2_norm, ffn2_w_gate, ffn2_w_up,
              ffn2_w_down, final_norm):
    c_act = jax.nn.silu(c)
    for i in range(DEPTH):
        mod = c_act @ w_ada[i] + b_ada[i]
        sh1, sc1, g1, sh2, sc2, g2, sh3, sc3, g3 = jnp.split(mod, N_MOD, axis=-1)
        h = modulate(rms_norm(x, ffn1_norm[i]), sh1, sc1)
        x = x + FFN_RES * g1[:, None, :] * swiglu(h, ffn1_w_gate[i], ffn1_w_up[i], ffn1_w_down[i])
        h = modulate(rms_norm(x, mix_norm[i]), sh2, sc2)
        x = x + g2[:, None, :] * token_mixer(
            h, w_in[i], conv_w[i], conv_b[i], dt_bias[i], a_log[i], d_skip[i], ssd_norm_w[i],
            pool_w[i], pool_b[i], pool_scale[i], w_out[i])
        h = modulate(rms_norm(x, ffn2_norm[i]), sh3, sc3)
        x = x + FFN_RES * g3[:, None, :] * swiglu(h, ffn2_w_gate[i], ffn2_w_up[i], ffn2_w_down[i])
    return rms_norm(x, final_norm)
```

```python
from contextlib import ExitStack

import numpy as np
import concourse.bass as bass
import concourse.mybir as mybir
from concourse.bass_utils import run_bass_kernel_spmd

F32 = mybir.dt.float32
BF16 = mybir.dt.bfloat16
AF = mybir.ActivationFunctionType
ALU = mybir.AluOpType

D = 1024
L = 4096
DFF = 2816
NF = DFF // 128
DIN = 4112
EPS = 1e-6
TP = 1024
NBLK = TP // 512
NCH = TP // 128
NEG = -30000.0

COMPUTE = ("pe", "act", "dve", "pool")
TRACKED_DRAM = set()


def _esize(dt):
    return 4 if dt in (F32, mybir.dt.int32, mybir.dt.uint32) else 2


def _region(ap):
    tn = type(ap.tensor).__name__
    if tn.startswith("DRam"):
        if ap.tensor.name in TRACKED_DRAM:
            return (ap.tensor.name, 0, 1, 0, 1)
        return None
    pat = ap.ap
    pstep, pcnt = pat[0]
    off = int(ap.offset)
    es = _esize(ap.dtype)
    if pstep == 0:
        p0, f0 = 0, off
    else:
        p0 = off // pstep
        f0 = off - p0 * pstep
    ext = 0
    for st, cn in pat[1:]:
        ext += abs(st) * (cn - 1)
    if tn.startswith("PSum"):
        return (ap.tensor.name, p0, p0 + pcnt, 0, 2048)
    return (ap.tensor.name, p0, p0 + pcnt, f0 * es, (f0 + ext + 1) * es)


class Op:
    __slots__ = ("eng", "fn", "deps", "marked", "value", "idx", "is_dma", "sem", "desc")

    def __init__(self, eng, fn, is_dma, desc):
        self.eng = eng
        self.fn = fn
        self.deps = set()
        self.marked = False
        self.value = None
        self.is_dma = is_dma
        self.sem = None
        self.desc = desc


class Prog:
    def __init__(self, nc, same_engine_sync=True, dma_ring=12):
        self.nc = nc
        self.ops = []
        self.same_engine_sync = same_engine_sync
        self.dma_ring = dma_ring
        self.track = {}
        self.out_dma_ops = []

    def add(self, eng, fn, reads=(), writes=(), is_dma=False, desc=""):
        op = Op(eng, fn, is_dma, desc)
        op.idx = len(self.ops)
        self.ops.append(op)
        for ap in reads:
            if ap is None or isinstance(ap, (int, float)):
                continue
            r = _region(ap)
            if r is not None:
                self._access(op, r, False)
        for ap in writes:
            r = _region(ap)
            if r is not None:
                self._access(op, r, True)
        return op

    def _access(self, op, r, is_write):
        name, p0, p1, f0, f1 = r
        lst = self.track.get(name, [])
        keep = []
        for rec in lst:
            q0, q1, g0, g1, prev, pw = rec
            if q1 <= p0 or p1 <= q0 or g1 <= f0 or f1 <= g0:
                keep.append(rec)
                continue
            if (is_write or pw) and prev is not op:
                op.deps.add(prev)
            covered = q0 >= p0 and q1 <= p1 and g0 >= f0 and g1 <= f1
            if is_write and covered:
                continue
            if (not is_write) and (not pw) and covered and prev.eng == op.eng and not prev.is_dma and not op.is_dma:
                continue
            keep.append(rec)
        keep.append((p0, p1, f0, f1, op, is_write))
        self.track[name] = keep

    def _skip(self, d, op):
        if d.eng == op.eng and not d.is_dma and not op.is_dma:
            if d.eng == "pe" or not self.same_engine_sync:
                return True
        return False

    def emit(self, stack):
        nc = self.nc
        engs = {"pe": nc.tensor, "act": nc.scalar, "dve": nc.vector, "pool": nc.gpsimd, "sp": nc.sync}
        for op in self.ops:
            for d in op.deps:
                if not self._skip(d, op):
                    d.marked = True
        sems = {e: stack.enter_context(nc.semaphore("s_" + e)) for e in COMPUTE}
        rings = {e: [stack.enter_context(nc.semaphore("d_%s%d" % (e, i))) for i in range(self.dma_ring)]
                 for e in ("sp", "pool")}
        cnt = {e: 0 for e in COMPUTE}
        dcnt = {e: 0 for e in rings}
        per_eng = {e: [] for e in engs}
        for op in self.ops:
            per_eng[op.eng].append(op)
            if op.is_dma:
                i = dcnt[op.eng]
                dcnt[op.eng] += 1
                op.sem = rings[op.eng][i % self.dma_ring]
                op.value = 16 * (i // self.dma_ring + 1)
                op.marked = True
            elif op.marked:
                cnt[op.eng] += 1
                op.sem = sems[op.eng]
                op.value = cnt[op.eng]
        self.stats = dict(cnt=cnt, dcnt=dcnt, nops=len(self.ops))
        block = stack.enter_context(nc.Block())

        def make(ename):
            ops = per_eng[ename]

            def body(e):
                waited = {}
                for op in ops:
                    need = {}
                    for d in op.deps:
                        if d.sem is None or self._skip(d, op):
                            continue
                        k = d.sem.num
                        if need.get(k, (None, 0))[1] < d.value:
                            need[k] = (d.sem, d.value)
                    if op.is_dma and op.value > 16:
                        k = op.sem.num
                        if need.get(k, (None, 0))[1] < op.value - 16:
                            need[k] = (op.sem, op.value - 16)
                    for k, (s, v) in need.items():
                        if waited.get(k, 0) >= v:
                            continue
                        e.wait_ge(s, v)
                        waited[k] = v
                    ins = op.fn(e)
                    if op.is_dma:
                        ins.then_inc(op.sem, 16)
                    elif op.marked:
                        ins.then_inc(op.sem, 1)
                if ename == "sp":
                    for op in self.out_dma_ops:
                        e.wait_ge(op.sem, op.value)
            return body

        block.sync(make("sp"))
        block.tensor(make("pe"))
        block.scalar(make("act"))
        block.vector(make("dve"))
        block.gpsimd(make("pool"))

    def mm(self, out, lhsT, rhs, start=True, stop=True):
        return self.add("pe", lambda e: e.matmul(out, lhsT, rhs, start=start, stop=stop),
                        reads=[lhsT, rhs], writes=[out], desc="mm")

    def transpose(self, out, in_, ident):
        return self.add("pe", lambda e: e.transpose(out, in_, ident), reads=[in_, ident], writes=[out], desc="tr")

    def act(self, out, in_, func, bias=None, scale=1.0, accum_out=None):
        kw = {"scale": scale}
        rd = [in_]
        wr = [out]
        if bias is not None:
            kw["bias"] = bias
            if not isinstance(bias, (int, float)):
                rd.append(bias)
        if not isinstance(scale, (int, float)):
            rd.append(scale)
        if accum_out is not None:
            kw["accum_out"] = accum_out
            wr.append(accum_out)
        return self.add("act", lambda e: e.activation(out, in_, func, **kw), reads=rd, writes=wr, desc="act")

    def tt(self, eng, out, in0, in1, op):
        return self.add(eng, lambda e: e.tensor_tensor(out, in0, in1, op), reads=[in0, in1], writes=[out], desc="tt")

    def ts(self, eng, out, in0, s1, s2, op0, op1=None):
        rd = [in0] + [s for s in (s1, s2) if s is not None and not isinstance(s, (int, float))]
        if op1 is None:
            return self.add(eng, lambda e: e.tensor_scalar(out, in0, s1, None, op0), reads=rd, writes=[out], desc="ts")
        return self.add(eng, lambda e: e.tensor_scalar(out, in0, s1, s2, op0, op1), reads=rd, writes=[out], desc="ts")

    def stt(self, eng, out, in0, scalar, in1, op0, op1):
        rd = [in0, in1] + ([scalar] if not isinstance(scalar, (int, float)) else [])
        return self.add(eng, lambda e: e.scalar_tensor_tensor(out, in0, scalar, in1, op0, op1),
                        reads=rd, writes=[out], desc="stt")

    def stt_acc(self, eng, out, in0, scalar, in1, op0, op1, accum_out):
        rd = [in0, in1] + ([scalar] if not isinstance(scalar, (int, float)) else [])
        return self.add(eng, lambda e: e.scalar_tensor_tensor(out, in0, scalar, in1, op0, op1, accum_out=accum_out),
                        reads=rd, writes=[out, accum_out], desc="stta")

    def copy(self, eng, out, in_):
        if eng == "act":
            return self.add(eng, lambda e: e.copy(out, in_), reads=[in_], writes=[out], desc="copy")
        return self.add(eng, lambda e: e.tensor_copy(out, in_), reads=[in_], writes=[out], desc="copy")

    def memset(self, eng, out, val):
        return self.add(eng, lambda e: e.memset(out, val), writes=[out], desc="memset")

    def recip(self, out, in_):
        return self.add("dve", lambda e: e.reciprocal(out, in_), reads=[in_], writes=[out], desc="recip")

    def dma(self, eng, out, in_, is_output=False):
        op = self.add(eng, lambda e: e.dma_start(out, in_), reads=[in_], writes=[out], is_dma=True, desc="dma")
        if is_output:
            self.out_dma_ops.append(op)
        return op


class Arena:
    def __init__(self, arena_ap, nbytes):
        self.a = arena_ap
        self.nbytes = nbytes
        self.top = 0

    def alloc(self, shape, dtype, at=None):
        es = _esize(dtype)
        n = 1
        for s in shape:
            n *= s
        nb = (n * es + 63) // 64 * 64
        if at is None:
            at = self.top
            self.top += nb
        assert at + nb <= self.nbytes, ("SBUF arena overflow", at, nb, self.nbytes)
        v = self.a[:, at // 4: (at + nb) // 4]
        if dtype != F32:
            v = v.bitcast(dtype)
        v = v[:, 0:n]
        if len(shape) == 2:
            v = v.rearrange("p (a b) -> p a b", a=shape[0])
        elif len(shape) == 3:
            v = v.rearrange("p (a b c) -> p a b c", a=shape[0], b=shape[1])
        return v, at, nb


def build_program(n_pass, debug=False, n_pre=0):
    nc = bass.Bass("TRN2", target_bir_lowering=False)
    LT = n_pass * TP
    LA = max(n_pre, 1) * TP

    def din(name, shape):
        return nc.dram_tensor(name, list(shape), F32, kind="ExternalInput").ap()

    xT = din("xT", [D, LT])
    xTa = din("xTa", [D, LA])
    role_d = din("role", [128, 1])
    c_bc = din("c_bc", [128, D])
    w_adaT = din("w_adaT", [128, 72, D])
    b_ada = din("b_ada", [128, 72])
    nw_d = din("nw", [128, 32])
    wg_d = [din("wg1", [D, DFF]), din("wg2", [D, DFF])]
    wu_d = [din("wu1", [D, DFF]), din("wu2", [D, DFF])]
    wd_d = [din("wd1", [DFF, D]), din("wd2", [DFF, D])]
    w_in = din("w_in", [D, DIN])
    cw_d = din("conv_w", [128, 64])
    cb_d = din("conv_b", [128, 16])
    hp_d = din("headp", [128, 48])
    snw_d = din("ssd_norm_w", [128, D])
    pw_d = din("pool_w", [4, 256, 256])
    pb_d = din("pool_b", [128, 8])
    psc_d = din("pool_scale", [128, 8])
    w_out = din("w_out", [2 * D, D])
    cf_d = din("constf", [128, 3 * 128])
    mask_d = din("maskT", [128, 4 * 128])
    pm_d = din("poolM", [128, 12 * 128])
    outT = nc.dram_tensor("outT", [D, LT], F32, kind="ExternalOutput").ap()
    dbg = []
    if debug:
        dbg = [nc.dram_tensor("dbg%d" % i, [D, TP], F32, kind="ExternalOutput").ap() for i in range(3)]

    xT_v = xT.rearrange("(k p) t -> p k t", p=128)
    xTa_v = xTa.rearrange("(k p) t -> p k t", p=128)
    outT_v = outT.rearrange("(k p) t -> p k t", p=128)
    w_in_v = w_in.rearrange("(k p) f -> p k f", p=128)
    w_out_v = w_out.rearrange("(k p) f -> p k f", p=128)

    with ExitStack() as st:
        E = st.enter_context
        ARENA_BYTES = 206 * 1024
        arena_t = E(nc.sbuf_tensor("arena", [128, ARENA_BYTES // 4], F32))
        A = Arena(arena_t[:], ARENA_BYTES)
        ps = [E(nc.psum_tensor("ps%d" % i, [128, 512], F32)) for i in range(8)]
        psf = [p[:] for p in ps]
        psb = [p[:].bitcast(BF16) for p in ps]
        P = Prog(nc)

        def al(shape, dtype):
            return A.alloc(shape, dtype)[0]

        constf = al([3, 128], F32)
        ident_f, tri_f, ones_f = constf[:, 0, :], constf[:, 1, :], constf[:, 2, :]
        ident_b = al([128], BF16)
        ones_b = al([128], BF16)
        mask_b = al([4 * 128], BF16)
        poolM = al([12, 128], BF16)
        diag4 = [al([4, 128], BF16) for _ in range(2)]
        nw = al([4, 8], F32)
        mod = al([72], F32)
        Amod = al([3, 8], F32)
        gate = al([3, 8], F32)
        cw = al([16, 4], F32)
        cb = al([16], F32)
        headp = al([48], F32)
        aneg8 = al([NCH * 16], F32)
        dtb8 = al([NCH * 16], F32)
        snw = al([D], F32)
        poolw = al([4, 2, 256], BF16)
        pb = al([8], F32)
        psc = al([8], F32)
        pbs = al([8], F32)
        wdt = al([8, 16], BF16)
        epsc = al([1], F32)
        onec = al([1], F32)
        role = al([1], F32)
        prev = al([D], F32)
        prev_b2 = [al([D], BF16) for _ in range(2)]
        prev_b = prev_b2[0]
        uhalo = al([D], BF16)
        xhalo = al([16, 3], BF16)
        dtall = al([NCH * 16], F32)
        adt = al([NCH * 16], F32)
        acs = al([NCH * 16], F32)
        nacs = al([NCH * 16], F32)
        acs_hb = al([NCH * 16], BF16)
        sd = al([NCH * 16], F32)
        dtds = al([NCH * 16], F32)
        cdb = al([NCH * 16], F32)
        sp0 = al([NCH * 16], F32)
        sp1 = al([NCH * 16], F32)
        ssq = al([4], F32)
        grs = al([4], F32)
        xres = al([8, TP], F32)
        sqb = [al([512], BF16) for _ in range(2)]
        rstd2 = [al([512], F32) for _ in range(2)]
        ntmp = [al([512], F32) for _ in range(2)]
        PH = A.top

        A.top = PH
        hT = al([8, TP], BF16)
        actb = al([NF, TP], BF16)
        wgu = [al([2, 8, 256], BF16) for _ in range(2)]
        wdn = [al([NF, 256], BF16) for _ in range(2)]
        sgt = [al([512], BF16) for _ in range(2)]
        _keep = A.top
        A.top = PH
        ostage2 = [al([8, 512], F32), al([8, 512], F32)]
        A.top = _keep
        cact = al([D], F32)
        junk = al([D], F32)
        bada = al([72], F32)
        modacc = al([72], F32)
        aexp = al([16], F32)
        wsm = [al([D], F32) for _ in range(4)]
        ffn_top = A.top
        A.top = PH
        wada = [al([8, D], F32) for _ in range(2)]
        setup_top = A.top
        A.top = PH
        r1_at = A.top
        xbc_raw = al([16, 3 + TP], BF16)
        A.top = r1_at
        ycatT = al([16, TP], BF16)
        A.top = r1_at + (16 * (3 + TP) * 2 + 63) // 64 * 64
        xbc_c = al([16, TP], BF16)
        z_tok = al([NCH, D], BF16)
        X_at = A.top
        h2T = al([8, TP], BF16)
        u_tok = al([NCH + 1, D], BF16)
        wst = [al([8, 512], BF16) for _ in range(2)]
        inproj_top = A.top
        A.top = X_at + 16 * 1024 + (NCH + 1) * D * 2
        diffT = al([8, 512], BF16)
        A.top = X_at
        xdt = [al([D], BF16) for _ in range(2)]
        xdtds = [al([D], BF16) for _ in range(2)]
        xsD = [al([D], BF16) for _ in range(2)]
        Btok = [al([512], BF16) for _ in range(2)]
        Rhi = al([16, 128], BF16)
        Rlo = al([16, 128], BF16)
        decT = [al([16, 128], BF16) for _ in range(2)]
        scT = [al([16, 128], BF16) for _ in range(2)]
        t1 = al([D], F32)
        szb = al([D], F32)
        ynb = al([D], BF16)
        chunk_top = A.top
        A.top = X_at
        wo = [al([16, 256], BF16) for _ in range(2)]
        mix_top = max(inproj_top, chunk_top, A.top)
        assert max(ffn_top, setup_top, mix_top) <= ARENA_BYTES, (ffn_top, setup_top, mix_top)

        rot = {"a": 0}

        def alt():
            rot["a"] ^= 1
            return "act" if rot["a"] else "dve"

        def bc_last(ap2, n):
            return ap2.unsqueeze(2).to_broadcast([128, ap2.shape[1], n])

        def bc_mid(ap2, n):
            return ap2.unsqueeze(1).to_broadcast([128, n, ap2.shape[1]])

        P.dma("sp", constf, cf_d.rearrange("p (a b) -> p a b", a=3))
        P.dma("pool", mask_b, mask_d)
        P.dma("pool", poolM, pm_d.rearrange("p (a b) -> p a b", a=12))
        P.dma("sp", nw, nw_d.rearrange("p (a b) -> p a b", a=4))
        P.dma("sp", cw, cw_d.rearrange("p (a b) -> p a b", a=16))
        P.dma("sp", cb, cb_d)
        P.dma("sp", headp, hp_d)
        P.dma("sp", snw, snw_d)
        P.dma("sp", pb, pb_d)
        P.dma("sp", psc, psc_d)
        P.dma("sp", bada, b_ada)
        P.dma("sp", role, role_d)
        P.dma("sp", cact, c_bc)
        P.dma("pool", poolw, pw_d.rearrange("g (k c) d -> c g k d", c=128))
        P.dma("pool", wdt, w_in_v[:, :, 3072:3088])
        P.copy("dve", ident_b, ident_f)
        P.copy("dve", ones_b, ones_f)
        P.memset("dve", epsc, EPS)
        P.memset("dve", onec, 1.0)
        P.memset("dve", prev, 0.0)
        P.memset("dve", prev_b, 0.0)
        P.memset("dve", uhalo, 0.0)
        P.memset("dve", xhalo, 0.0)
        P.act(cact, cact, AF.Silu)
        def mod_j(j, wrow):
            P.stt_acc("dve", junk, wrow, 1.0, cact, ALU.mult, ALU.mult, modacc[:, j:j + 1])

        def mod_vec_done(v):
            vs = slice(v * 8, (v + 1) * 8)
            P.tt("dve", mod[:, vs], modacc[:, vs], bada[:, vs], ALU.add)
            i = v // 3
            if v % 3 == 1:
                P.stt("dve", Amod[:, i, :], mod[:, vs], 1.0, nw[:, i, :], ALU.add, ALU.mult)
            elif v % 3 == 2:
                P.ts("dve", gate[:, i, :], mod[:, vs], 1.0 if i == 1 else 0.5, None, ALU.mult)

        for ch in range(2):
            wb = wada[ch % 2]
            P.dma("sp", wb, w_adaT[:, ch * 8:(ch + 1) * 8, :])
            for jj in range(8):
                mod_j(ch * 8 + jj, wb[:, jj, :])
            mod_vec_done(ch)

        def bg_mod():
            for j in range(16, 72):
                bg["j"] = j + 1
                wb = wsm[j % 4]
                P.dma("sp", wb, w_adaT[:, j, :])
                mod_j(j, wb)
                if j % 8 == 7:
                    mod_vec_done(j // 8)
                yield

        bg = {"gen": None, "j": 16}
        bg["gen"] = bg_mod()

        def bg_step(n):
            for _ in range(n):
                if bg["gen"] is None:
                    return
                try:
                    next(bg["gen"])
                except StopIteration:
                    bg["gen"] = None
        P.act(aexp, headp[:, 16:32], AF.Exp)
        for c in range(NCH):
            P.ts("dve", aneg8[:, c * 16:(c + 1) * 16], aexp, -1.0, None, ALU.mult)
            P.copy("dve", dtb8[:, c * 16:(c + 1) * 16], headp[:, 0:16])
        P.tt("dve", pbs, pb, psc, ALU.mult)

        sqc = {"n": 0, "pend": None}

        def sumsq_rstd(blk, have_sums):
            bs = slice(blk * 512, (blk + 1) * 512)
            if not have_sums:
                for k in range(8):
                    P.act(sqb[k % 2], xres[:, k, bs], AF.Square)
                    P.mm(psf[6 + blk], ones_b, sqb[k % 2], start=(k == 0), stop=(k == 7))
            P.act(rstd2[blk], psf[6 + blk], AF.Sqrt, bias=epsc[:, 0:1], scale=1.0 / D)
            P.recip(rstd2[blk], rstd2[blk])

        def res_sumsq_flush():
            if sqc["pend"] is not None:
                blk, buf, first, last = sqc["pend"]
                P.mm(psf[6 + blk], ones_b, buf, start=first, stop=last)
                sqc["pend"] = None

        def res_sumsq(dtile, blk):
            res_sumsq_flush()
            buf = sqb[sqc["n"] % 2]
            sqc["n"] += 1
            P.act(buf, xres[:, dtile, blk * 512:(blk + 1) * 512], AF.Square)
            sqc["pend"] = (blk, buf, dtile == 0, dtile == 7)

        def norm_mod(i, dst, have_sums=False):
            for blk in range(NBLK):
                sumsq_rstd(blk, have_sums)
            for blk in range(NBLK):
                bs = slice(blk * 512, (blk + 1) * 512)
                for k in range(8):
                    P.stt("dve", ntmp[k % 2], xres[:, k, bs], Amod[:, i, k:k + 1], rstd2[blk], ALU.mult, ALU.mult)
                    P.act(dst[:, k, bs], ntmp[k % 2], AF.Identity, bias=mod[:, 3 * i * 8 + k:3 * i * 8 + k + 1], scale=1.0)

        cnt = {"g": 0, "d": 0, "x": 0, "c": 0}

        def ffn(fi):
            wg_v = wg_d[fi].rearrange("(k p) f -> p k f", p=128)
            wu_v = wu_d[fi].rearrange("(k p) f -> p k f", p=128)
            wd_v = wd_d[fi].rearrange("(f p) d -> p f d", p=128)
            i = 2 * fi
            norm_mod(i, hT, have_sums=(fi == 1))
            for pr in range(NF // 2):
                slot = wgu[pr % 2]
                P.dma("pool", slot[:, 0], wg_v[:, :, pr * 256:(pr + 1) * 256])
                P.dma("pool", slot[:, 1], wu_v[:, :, pr * 256:(pr + 1) * 256])
                for sub in range(2):
                    j = pr * 2 + sub
                    for blk in range(NBLK):
                        bs = slice(blk * 512, (blk + 1) * 512)
                        x = cnt["g"] % 2
                        cnt["g"] += 1
                        pg, pu = psf[x], psf[2 + x]
                        for k in range(8):
                            P.mm(pg, slot[:, 0, k, sub * 128:(sub + 1) * 128], hT[:, k, bs], start=(k == 0), stop=(k == 7))
                        for k in range(8):
                            P.mm(pu, slot[:, 1, k, sub * 128:(sub + 1) * 128], hT[:, k, bs], start=(k == 0), stop=(k == 7))
                        P.act(sgt[x], pg, AF.Silu)
                        P.tt("dve", actb[:, j, bs], sgt[x], pu, ALU.mult)
                        bg_step(2 if (pr * 4 + sub * 2 + blk) < 4 else 1)
            if bg["gen"] is not None:
                for _ in range(8):
                    if bg["j"] < 24:
                        bg_step(1)
            for pr in range(4):
                slot = wdn[pr % 2]
                P.dma("pool", slot, wd_v[:, :, pr * 256:(pr + 1) * 256])
                for sub in range(2):
                    dtile = pr * 2 + sub
                    for blk in range(NBLK):
                        bs = slice(blk * 512, (blk + 1) * 512)
                        pd = psf[4 + cnt["d"] % 2]
                        cnt["d"] += 1
                        for f in range(NF):
                            P.mm(pd, slot[:, f, sub * 128:(sub + 1) * 128], actb[:, f, bs], start=(f == 0), stop=(f == NF - 1))
                        P.stt("dve", xres[:, dtile, bs], pd, gate[:, i, dtile:dtile + 1], xres[:, dtile, bs], ALU.mult, ALU.add)
                        res_sumsq(dtile, blk)
                        bg_step(1)
            res_sumsq_flush()
            bg_step(100)

        def nextps4():
            x = cnt["x"] % 4
            cnt["x"] += 1
            return x

        def mixer(pi, a1=False, last_pre=False):
            norm_mod(1, h2T, have_sums=True)
            P.copy("dve", xbc_raw[:, :, 0:3], xhalo)
            P.copy("dve", u_tok[:, 0, :], uhalo)
            def dt_block():
                for c in range(NCH):
                    cs = slice(c * 128, (c + 1) * 128)
                    for k in range(8):
                        P.mm(psf[4][:, c * 16:(c + 1) * 16], h2T[:, k, cs], wdt[:, k, :], start=(k == 0), stop=(k == 7))
                NH = NCH * 16
                P.tt("dve", sp0, psf[4][:, 0:NH], dtb8, ALU.add)
                P.ts("dve", sp1, sp0, -1.0, None, ALU.mult)
                P.tt("dve", sp1, sp1, sp0, ALU.min)
                P.act(sp1, sp1, AF.Exp)
                P.act(sp1, sp1, AF.Ln, bias=onec[:, 0:1], scale=1.0)
                P.ts("dve", sp0, sp0, 0.0, None, ALU.max)
                P.tt("dve", dtall, sp0, sp1, ALU.add)
                P.tt("dve", adt, dtall, aneg8, ALU.mult)
                P.mm(psf[5][:, 0:NH], tri_f, adt)
                P.mm(psf[5][:, NH:2 * NH], ones_f, adt)
                P.copy("dve", acs, psf[5][:, 0:NH])
                P.ts("dve", nacs, acs, -1.0, None, ALU.mult)
                P.act(sd, acs, AF.Exp)
                P.tt("dve", sp0, psf[5][:, NH:2 * NH], acs, ALU.subtract)
                P.act(sp0, sp0, AF.Exp)
                P.tt("dve", dtds, dtall, sp0, ALU.mult)
                P.act(cdb, psf[5][:, NH:2 * NH], AF.Exp)
                P.copy("dve", acs_hb, acs)
                P.copy("dve", sp1, acs_hb)
                P.tt("dve", sp0, acs, sp1, ALU.subtract)

            for pr in range(8):
                if pr == 3:
                    dt_block()
                if a1 and pr >= 6 and not last_pre:
                    continue
                slot = wst[pr % 2]
                P.dma("pool", slot[:, :, 0:256], w_in_v[:, :, 1024 + pr * 256:1024 + (pr + 1) * 256])
                for sub in range(2):
                    i = pr * 2 + sub
                    for blk in range(NBLK):
                        if a1 and pr >= 6 and blk != NBLK - 1:
                            continue
                        bs = slice(blk * 512, (blk + 1) * 512)
                        pp = psf[nextps4()]
                        for k in range(8):
                            P.mm(pp, slot[:, k, sub * 128:(sub + 1) * 128], h2T[:, k, bs], start=(k == 0), stop=(k == 7))
                        P.copy(alt(), xbc_raw[:, i, 3 + blk * 512:3 + (blk + 1) * 512], pp)
            for grp in range(4):
                if a1 and (grp < 2 or not last_pre):
                    continue
                slot = wst[grp % 2]
                col0 = (0, 512, 3088, 3600)[grp]
                P.dma("pool", slot, w_in_v[:, :, col0:col0 + 512])
                for c in (range(NCH - 1, NCH) if a1 else range(NCH)):
                    cs = slice(c * 128, (c + 1) * 128)
                    pp = psf[nextps4()]
                    for k in range(8):
                        P.mm(pp, h2T[:, k, cs], slot[:, k, :], start=(k == 0), stop=(k == 7))
                    if grp < 2:
                        P.act(z_tok[:, c, grp * 512:(grp + 1) * 512], pp, AF.Silu)
                    else:
                        P.copy(alt(), u_tok[:, c + 1, (grp - 2) * 512:(grp - 1) * 512], pp)
            for i in range(12 if a1 else 16):
                if i % 3 == 1:
                    for blk in range(NBLK):
                        acc = ntmp[blk % 2]
                        P.ts("dve", acc, xbc_raw[:, i, blk * 512:blk * 512 + 512], cw[:, i, 0:1], None, ALU.mult)
                        for k in range(1, 4):
                            P.stt("dve", acc, xbc_raw[:, i, blk * 512 + k:blk * 512 + k + 512], cw[:, i, k:k + 1], acc, ALU.mult, ALU.add)
                        P.act(xbc_c[:, i, blk * 512:(blk + 1) * 512], acc, AF.Silu, bias=cb[:, i:i + 1], scale=1.0)
                    continue
                dg = diag4[cnt["c"] % 2]
                cnt["c"] += 1
                for k in range(4):
                    P.ts("dve", dg[:, k, :], ident_f, cw[:, i, k:k + 1], None, ALU.mult)
                for blk in range(NBLK):
                    pp = psf[nextps4()]
                    for k in range(4):
                        P.mm(pp, dg[:, k, :], xbc_raw[:, i, blk * 512 + k:blk * 512 + k + 512], start=(k == 0), stop=(k == 3))
                    P.act(xbc_c[:, i, blk * 512:(blk + 1) * 512], pp, AF.Silu, bias=cb[:, i:i + 1], scale=1.0)
            P.copy("dve", xhalo, xbc_raw[:, :, TP:TP + 3])
            for blk in range(0 if a1 else NBLK):
                bs = slice(blk * 512, (blk + 1) * 512)
                for i in range(8):
                    g = i // 2
                    pp = psf[i % 2]
                    for cc in range(4):
                        c = blk * 4 + cc
                        first = (pi == 0 and c == 0)
                        Mx = poolM[:, g, :] if first else poolM[:, 4 + g, :]
                        P.mm(pp[:, cc * 128:(cc + 1) * 128], u_tok[:, c + 1, i * 128:(i + 1) * 128], Mx, start=True, stop=False)
                        P.mm(pp[:, cc * 128:(cc + 1) * 128], u_tok[:, c, i * 128:(i + 1) * 128], poolM[:, 8 + g, :], start=False, stop=True)
                    P.copy(alt(), diffT[:, i, :], pp)
                for ot in range(8):
                    g, kd = ot // 2, ot % 2
                    pp = psf[2 + ot % 2]
                    for kc in range(2):
                        P.mm(pp, poolw[:, g, kc, kd * 128:(kd + 1) * 128], diffT[:, g * 2 + kc, :], start=(kc == 0), stop=(kc == 1))
                    P.act(ycatT[:, 8 + ot, bs], pp, AF.Identity, bias=pbs[:, ot:ot + 1], scale=psc[:, ot:ot + 1])
            if (not a1) or last_pre:
                P.copy("dve", uhalo, u_tok[:, NCH, :])
            def a1_front(c):
                cs = slice(c * 128, (c + 1) * 128)
                hs = slice(c * 16, (c + 1) * 16)
                par = c % 2
                tb0, tb1 = (0, 1) if par == 0 else (2, 4)
                for i in range(8):
                    P.transpose(psb[tb0][:, i * 128:(i + 1) * 128], xbc_c[:, i, cs], ident_b)
                for i in range(4):
                    P.transpose(psb[tb1][:, i * 128:(i + 1) * 128], xbc_c[:, 8 + i, cs], ident_b)
                P.tt("dve", xdtds[par].rearrange("p (h q) -> p h q", h=16), psb[tb0].rearrange("p (h q) -> p h q", h=16),
                     bc_last(dtds[:, hs], 64), ALU.mult)
                P.copy("act", Btok[par], psb[tb1][:, 0:512])

            def a1_back(c):
                hs = slice(c * 16, (c + 1) * 16)
                par = c % 2
                sb0, sb1 = (7, 3) if par == 0 else (5, 6)
                for g in range(4):
                    bank = sb0 if g < 2 else sb1
                    P.mm(psf[bank][:, (g % 2) * 256:(g % 2 + 1) * 256], Btok[par][:, g * 128:(g + 1) * 128],
                         xdtds[par][:, g * 256:(g + 1) * 256])
                P.tt("dve", prev.rearrange("p (h q) -> p h q", h=16), prev.rearrange("p (h q) -> p h q", h=16),
                     bc_last(cdb[:, hs], 64), ALU.mult)
                for b2 in range(2):
                    bank = sb0 if b2 == 0 else sb1
                    P.tt("dve", prev[:, b2 * 512:(b2 + 1) * 512], prev[:, b2 * 512:(b2 + 1) * 512], psf[bank], ALU.add)

            if a1:
                a1_front(0)
                for c in range(NCH):
                    if c + 1 < NCH:
                        a1_front(c + 1)
                    a1_back(c)
            def front(c):
                q = c % 2
                cs = slice(c * 128, (c + 1) * 128)
                hs = slice(c * 16, (c + 1) * 16)
                for i in range(8):
                    P.transpose(psb[0][:, i * 128:(i + 1) * 128], xbc_c[:, i, cs], ident_b)
                for i in range(4):
                    P.transpose(psb[1][:, i * 128:(i + 1) * 128], xbc_c[:, 8 + i, cs], ident_b)
                cb4 = psf[2].rearrange("p (g l) -> p g l", g=4)
                for g in range(4):
                    P.mm(cb4[:, g, :], xbc_c[:, 8 + g, cs], xbc_c[:, 12 + g, cs])
                P.tt("pool", Rhi, bc_mid(ident_f, 16), bc_last(sp1[:, hs], 128), ALU.mult)
                P.tt("pool", Rlo, bc_mid(ident_f, 16), bc_last(sp0[:, hs], 128), ALU.mult)
                yield
                xs3 = psb[0].rearrange("p (h q) -> p h q", h=16)
                P.tt("dve", xdt[q].rearrange("p (h q) -> p h q", h=16), xs3, bc_last(dtall[:, hs], 64), ALU.mult)
                P.tt("dve", xsD[q].rearrange("p (h q) -> p h q", h=16), xs3, bc_last(headp[:, 32:48], 64), ALU.mult)
                P.tt("dve", xdtds[q].rearrange("p (h q) -> p h q", h=16), xs3, bc_last(dtds[:, hs], 64), ALU.mult)
                P.copy("act", Btok[q], psb[1][:, 0:512])
                yield
                for g in range(4):
                    if g == 1:
                        P.tt("pool", prev.rearrange("p (h q) -> p h q", h=16), prev.rearrange("p (h q) -> p h q", h=16),
                             bc_last(cdb[:, hs], 64), ALU.mult)
                        for b2 in range(2):
                            for g_ in (2 * b2, 2 * b2 + 1):
                                P.mm(psf[7][:, (g_ % 2) * 256:(g_ % 2 + 1) * 256], Btok[q][:, g_ * 128:(g_ + 1) * 128],
                                     xdtds[q][:, g_ * 256:(g_ + 1) * 256])
                            P.tt("dve", prev[:, b2 * 512:(b2 + 1) * 512], prev[:, b2 * 512:(b2 + 1) * 512], psf[7], ALU.add)
                        P.copy("act", prev_b2[(c + 1) % 2], prev)
                        yield
                    sp_ = psf[g % 2]
                    P.mm(sp_, ones_b, Rhi[:, g * 4:(g + 1) * 4, :].rearrange("p a b -> p (a b)"), start=True, stop=False)
                    P.mm(sp_, ones_b, Rlo[:, g * 4:(g + 1) * 4, :].rearrange("p a b -> p (a b)"), start=False, stop=False)
                    P.mm(sp_, ident_b, mask_b, start=False, stop=True)
                    sp3 = sp_.rearrange("p (a b) -> p a b", a=4)
                    for r in range(4):
                        h = g * 4 + r
                        P.act(decT[q][:, h, :], sp3[:, r, :], AF.Exp, bias=nacs[:, c * 16 + h:c * 16 + h + 1], scale=1.0)
                    P.tt("dve", scT[q][:, g * 4:(g + 1) * 4, :], decT[q][:, g * 4:(g + 1) * 4, :], bc_mid(cb4[:, g, :], 4), ALU.mult)
                    yield

            def back(c):
                q = c % 2
                cs = slice(c * 128, (c + 1) * 128)
                hs = slice(c * 16, (c + 1) * 16)
                for g in range(4):
                    P.mm(psf[3 + g // 2][:, (g % 2) * 256:(g % 2 + 1) * 256], xbc_c[:, 12 + g, cs], prev_b2[c % 2][:, g * 256:(g + 1) * 256])
                for b2 in range(2):
                    P.mm(psf[5 + b2], ident_b, xsD[q][:, b2 * 512:(b2 + 1) * 512], start=True, stop=False)
                    for hh in range(8):
                        h = b2 * 8 + hh
                        P.mm(psf[5 + b2][:, hh * 64:(hh + 1) * 64], scT[q][:, h, :], xdt[q][:, h * 64:(h + 1) * 64], start=False, stop=(hh == 7))
                yield
                for b2 in range(2):
                    t13 = t1[:, b2 * 512:(b2 + 1) * 512].rearrange("p (h q) -> p h q", h=8)
                    P.tt("dve", t13, psf[3 + b2].rearrange("p (h q) -> p h q", h=8),
                         bc_last(sd[:, c * 16 + b2 * 8:c * 16 + b2 * 8 + 8], 64), ALU.mult)
                yield
                for b2 in range(2):
                    P.tt("dve", t1[:, b2 * 512:(b2 + 1) * 512], t1[:, b2 * 512:(b2 + 1) * 512], psf[5 + b2], ALU.add)
                yield
                P.tt("dve", t1, t1, z_tok[:, c, :], ALU.mult)
                for g in range(4):
                    gs = slice(g * 256, (g + 1) * 256)
                    P.stt_acc("dve", szb[:, gs], t1[:, gs], 1.0, t1[:, gs], ALU.mult, ALU.mult, ssq[:, g:g + 1])
                yield
                P.act(grs, ssq, AF.Ln, bias=epsc[:, 0:1], scale=1.0 / 256)
                P.act(grs, grs, AF.Exp, scale=-0.5)
                for g in range(4):
                    gs = slice(g * 256, (g + 1) * 256)
                    P.stt("dve", ynb[:, gs], t1[:, gs], grs[:, g:g + 1], snw[:, gs], ALU.mult, ALU.mult)
                yield
                for i in range(8):
                    P.transpose(psb[3][:, i * 128:(i + 1) * 128], ynb[:, i * 128:(i + 1) * 128], ident_b)
                P.copy("act", ycatT[:, 0:8, cs], psb[3].rearrange("p (a b) -> p a b", a=8))
                yield

            def drain(*gens):
                gens = list(gens)
                while gens:
                    for g_ in list(gens):
                        try:
                            next(g_)
                        except StopIteration:
                            gens.remove(g_)

            if not a1:
                P.copy("act", prev_b2[0], prev)
                drain(front(0))
                for c in range(NCH):
                    if c + 1 < NCH:
                        drain(front(c + 1), back(c))
                    else:
                        drain(back(c))
            for pr in range(0 if a1 else 4):
                slot = wo[pr % 2]
                P.dma("pool", slot, w_out_v[:, :, pr * 256:(pr + 1) * 256])
                for sub in range(2):
                    dtile = pr * 2 + sub
                    for blk in range(NBLK):
                        bs = slice(blk * 512, (blk + 1) * 512)
                        pp = psf[4 + cnt["d"] % 2]
                        cnt["d"] += 1
                        for k in range(16):
                            P.mm(pp, slot[:, k, sub * 128:(sub + 1) * 128], ycatT[:, k, bs], start=(k == 0), stop=(k == 15))
                        P.stt("dve", xres[:, dtile, bs], pp, gate[:, 1, dtile:dtile + 1], xres[:, dtile, bs], ALU.mult, ALU.add)
                        res_sumsq(dtile, blk)
            res_sumsq_flush()

        def final(pi):
            for blk in range(NBLK):
                bs = slice(blk * 512, (blk + 1) * 512)
                sumsq_rstd(blk, True)
                for k in range(8):
                    P.stt("dve", ostage2[blk][:, k, :], xres[:, k, bs], nw[:, 3, k:k + 1], rstd2[blk], ALU.mult, ALU.mult)
                P.dma("sp", outT_v[:, :, pi * TP + blk * 512:pi * TP + (blk + 1) * 512], ostage2[blk], is_output=True)
                if pi + 1 < n_pass:
                    P.dma("sp", xres[:, :, bs], xT_v[:, :, (pi + 1) * TP + blk * 512:(pi + 1) * TP + (blk + 1) * 512])

        def dump(i):
            if debug:
                P.dma("sp", dbg[i].rearrange("(k p) t -> p k t", p=128), xres, is_output=True)

        for pa in range(n_pre):
            for blk in range(NBLK):
                P.dma("sp", xres[:, :, blk * 512:(blk + 1) * 512], xTa_v[:, :, pa * TP + blk * 512:pa * TP + (blk + 1) * 512])
            ffn(0)
            mixer(1, a1=True, last_pre=(pa == n_pre - 1))
        if n_pre:
            P.ts("dve", prev, prev, role[:, 0:1], None, ALU.mult)
            P.copy("act", prev_b, prev)
            P.ts("dve", xhalo, xhalo, role[:, 0:1], None, ALU.mult)
            P.ts("dve", uhalo, uhalo, role[:, 0:1], None, ALU.mult)
        for pi in range(n_pass):
            if pi == 0:
                for blk in range(NBLK):
                    P.dma("sp", xres[:, :, blk * 512:(blk + 1) * 512], xT_v[:, :, blk * 512:(blk + 1) * 512])
            ffn(0)
            if pi == 0:
                dump(0)
            mixer(pi)
            if pi == 0:
                dump(1)
            ffn(1)
            if pi == 0:
                dump(2)
            final(pi)
        P.emit(st)
    return nc, P.stats


def _consts(first_is_seq_start):
    ident = np.eye(128, dtype=np.float32)
    tri = np.triu(np.ones((128, 128), np.float32))
    ones = np.ones((128, 128), np.float32)
    constf = np.concatenate([ident, tri, ones], axis=1)
    s = np.arange(128)[:, None]
    l = np.arange(128)[None, :]
    maskT = np.where(l >= s, 0.0, NEG).astype(np.float32)
    maskT = np.tile(maskT, (1, 4))
    mats = []
    windows = (2, 4, 8, 16)
    t = np.arange(128)[None, :]
    for kind in ("first", "diag", "off"):
        for w in windows:
            if kind == "off":
                sp_ = np.arange(128)[:, None] - 128
                m = np.where((t - sp_ >= 0) & (t - sp_ <= w - 1), 1.0 / w, 0.0)
            else:
                if kind == "first" and first_is_seq_start:
                    cntv = np.minimum(t + 1, w).astype(np.float64)
                else:
                    cntv = np.full_like(t, w, dtype=np.float64)
                m = np.where((t - s >= 0) & (t - s <= w - 1), 1.0 / cntv, 0.0) - (s == t)
            mats.append(m.astype(np.float32))
    poolM = np.concatenate(mats, axis=1)
    return constf, maskT, poolM


_CACHE = {}


def _prep_inputs(b, t0, n_tok, seq_start, inp, n_pre_tok=0):
    f = np.float32
    c = np.ascontiguousarray
    constf, maskT, poolM = _consts(seq_start)
    nwv = np.stack([inp["ffn1_norm"][0], inp["mix_norm"][0], inp["ffn2_norm"][0], inp["final_norm"]])
    m = {
        "xT": c(inp["x"][b, t0:t0 + n_tok].T),
        "xTa": c(inp["x"][b, 0:t0].T) if t0 > 0 else np.zeros((D, max(n_pre_tok, TP)), f),
        "role": np.full((128, 1), 1.0 if t0 > 0 else 0.0, f),
        "c_bc": c(np.broadcast_to(inp["c"][b], (128, D))),
        "w_adaT": _CACHE["w_adaT"],
        "b_ada": c(inp["b_ada"][0].reshape(72, 128).T),
        "nw": c(nwv.reshape(4, 8, 128).transpose(2, 0, 1).reshape(128, 32)),
        "wg1": _CACHE["wg1"], "wu1": _CACHE["wu1"], "wd1": _CACHE["wd1"],
        "wg2": _CACHE["wg2"], "wu2": _CACHE["wu2"], "wd2": _CACHE["wd2"],
        "w_in": _CACHE["w_in"],
        "conv_w": c(inp["conv_w"][0].reshape(4, 16, 128).transpose(2, 1, 0).reshape(128, 64)),
        "conv_b": c(inp["conv_b"][0].reshape(16, 128).T),
        "headp": c(np.broadcast_to(np.concatenate([inp["dt_bias"][0], inp["a_log"][0], inp["d_skip"][0]]), (128, 48))),
        "ssd_norm_w": c(np.broadcast_to(inp["ssd_norm_w"][0], (128, D))),
        "pool_w": _CACHE["pool_w"],
        "pool_b": c(inp["pool_b"][0].reshape(8, 128).T),
        "pool_scale": c(inp["pool_scale"][0].reshape(8, 128).T),
        "w_out": _CACHE["w_out"],
        "constf": constf, "maskT": maskT, "poolM": poolM,
    }
    return {k: np.asarray(v, dtype=f) for k, v in m.items()}


def kernel(**inputs):
    inp = {k: np.asarray(v) for k, v in inputs.items()}
    c = np.ascontiguousarray
    _CACHE["w_adaT"] = c(inp["w_ada"][0].T.reshape(72, 128, D).transpose(1, 0, 2))
    for nm, key in (("wg1", "ffn1_w_gate"), ("wu1", "ffn1_w_up"), ("wd1", "ffn1_w_down"),
                    ("wg2", "ffn2_w_gate"), ("wu2", "ffn2_w_up"), ("wd2", "ffn2_w_down"),
                    ("w_in", "w_in"), ("pool_w", "pool_w"), ("w_out", "w_out")):
        _CACHE[nm] = c(inp[key][0])
    H = L // 2
    n_pass = H // TP
    nc, stats = build_program(n_pass, n_pre=n_pass)
    in_maps = [_prep_inputs(core // 2, (core % 2) * H, H, core % 2 == 0, inp, n_pre_tok=H) for core in range(8)]
    res = run_bass_kernel_spmd(nc, in_maps, core_ids=list(range(8)))
    out = np.empty((4, L, D), np.float32)
    for core in range(8):
        out[core // 2, (core % 2) * H:(core % 2 + 1) * H] = res.results[core]["outT"].T
    return out
```

```python
from contextlib import ExitStack

import numpy as np
import concourse.bass as bass
import concourse.mybir as mybir
from concourse.bass_utils import run_bass_kernel_spmd

F32 = mybir.dt.float32
BF16 = mybir.dt.bfloat16
AF = mybir.ActivationFunctionType
ALU = mybir.AluOpType

D = 1024
L = 4096
DFF = 2816
NF = DFF // 128
DIN = 4112
EPS = 1e-6
TP = 1024
NBLK = TP // 512
NCH = TP // 128
NEG = -30000.0

COMPUTE = ("pe", "act", "dve", "pool")
TRACKED_DRAM = set()


def _esize(dt):
    return 4 if dt in (F32, mybir.dt.int32, mybir.dt.uint32) else 2


def _region(ap):
    tn = type(ap.tensor).__name__
    if tn.startswith("DRam"):
        if ap.tensor.name in TRACKED_DRAM:
            return (ap.tensor.name, 0, 1, 0, 1)
        return None
    pat = ap.ap
    pstep, pcnt = pat[0]
    off = int(ap.offset)
    es = _esize(ap.dtype)
    if pstep == 0:
        p0, f0 = 0, off
    else:
        p0 = off // pstep
        f0 = off - p0 * pstep
    ext = 0
    for st, cn in pat[1:]:
        ext += abs(st) * (cn - 1)
    if tn.startswith("PSum"):
        return (ap.tensor.name, p0, p0 + pcnt, 0, 2048)
    return (ap.tensor.name, p0, p0 + pcnt, f0 * es, (f0 + ext + 1) * es)


class Op:
    __slots__ = ("eng", "fn", "deps", "marked", "value", "idx", "is_dma", "sem", "desc")

    def __init__(self, eng, fn, is_dma, desc):
        self.eng = eng
        self.fn = fn
        self.deps = set()
        self.marked = False
        self.value = None
        self.is_dma = is_dma
        self.sem = None
        self.desc = desc


class Prog:
    def __init__(self, nc, same_engine_sync=True, dma_ring=12):
        self.nc = nc
        self.ops = []
        self.same_engine_sync = same_engine_sync
        self.dma_ring = dma_ring
        self.track = {}
        self.out_dma_ops = []

    def add(self, eng, fn, reads=(), writes=(), is_dma=False, desc=""):
        op = Op(eng, fn, is_dma, desc)
        op.idx = len(self.ops)
        self.ops.append(op)
        for ap in reads:
            if ap is None or isinstance(ap, (int, float)):
                continue
            r = _region(ap)
            if r is not None:
                self._access(op, r, False)
        for ap in writes:
            r = _region(ap)
            if r is not None:
                self._access(op, r, True)
        return op

    def _access(self, op, r, is_write):
        name, p0, p1, f0, f1 = r
        lst = self.track.get(name, [])
        keep = []
        for rec in lst:
            q0, q1, g0, g1, prev, pw = rec
            if q1 <= p0 or p1 <= q0 or g1 <= f0 or f1 <= g0:
                keep.append(rec)
                continue
            if (is_write or pw) and prev is not op:
                op.deps.add(prev)
            covered = q0 >= p0 and q1 <= p1 and g0 >= f0 and g1 <= f1
            if is_write and covered:
                continue
            if (not is_write) and (not pw) and covered and prev.eng == op.eng and not prev.is_dma and not op.is_dma:
                continue
            keep.append(rec)
        keep.append((p0, p1, f0, f1, op, is_write))
        self.track[name] = keep

    def _skip(self, d, op):
        if d.eng == op.eng and not d.is_dma and not op.is_dma:
            if d.eng == "pe" or not self.same_engine_sync:
                return True
        return False

    def emit(self, stack):
        nc = self.nc
        engs = {"pe": nc.tensor, "act": nc.scalar, "dve": nc.vector, "pool": nc.gpsimd, "sp": nc.sync}
        for op in self.ops:
            for d in op.deps:
                if not self._skip(d, op):
                    d.marked = True
        sems = {e: stack.enter_context(nc.semaphore("s_" + e)) for e in COMPUTE}
        rings = {e: [stack.enter_context(nc.semaphore("d_%s%d" % (e, i))) for i in range(self.dma_ring)]
                 for e in ("sp", "pool")}
        cnt = {e: 0 for e in COMPUTE}
        dcnt = {e: 0 for e in rings}
        per_eng = {e: [] for e in engs}
        for op in self.ops:
            per_eng[op.eng].append(op)
            if op.is_dma:
                i = dcnt[op.eng]
                dcnt[op.eng] += 1
                op.sem = rings[op.eng][i % self.dma_ring]
                op.value = 16 * (i // self.dma_ring + 1)
                op.marked = True
            elif op.marked:
                cnt[op.eng] += 1
                op.sem = sems[op.eng]
                op.value = cnt[op.eng]
        self.stats = dict(cnt=cnt, dcnt=dcnt, nops=len(self.ops))
        block = stack.enter_context(nc.Block())

        def make(ename):
            ops = per_eng[ename]

            def body(e):
                waited = {}
                for op in ops:
                    need = {}
                    for d in op.deps:
                        if d.sem is None or self._skip(d, op):
                            continue
                        k = d.sem.num
                        if need.get(k, (None, 0))[1] < d.value:
                            need[k] = (d.sem, d.value)
                    if op.is_dma and op.value > 16:
                        k = op.sem.num
                        if need.get(k, (None, 0))[1] < op.value - 16:
                            need[k] = (op.sem, op.value - 16)
                    for k, (s, v) in need.items():
                        if waited.get(k, 0) >= v:
                            continue
                        e.wait_ge(s, v)
                        waited[k] = v
                    ins = op.fn(e)
                    if op.is_dma:
                        ins.then_inc(op.sem, 16)
                    elif op.marked:
                        ins.then_inc(op.sem, 1)
                if ename == "sp":
                    for op in self.out_dma_ops:
                        e.wait_ge(op.sem, op.value)
            return body

        block.sync(make("sp"))
        block.tensor(make("pe"))
        block.scalar(make("act"))
        block.vector(make("dve"))
        block.gpsimd(make("pool"))

    def mm(self, out, lhsT, rhs, start=True, stop=True):
        return self.add("pe", lambda e: e.matmul(out, lhsT, rhs, start=start, stop=stop),
                        reads=[lhsT, rhs], writes=[out], desc="mm")

    def transpose(self, out, in_, ident):
        return self.add("pe", lambda e: e.transpose(out, in_, ident), reads=[in_, ident], writes=[out], desc="tr")

    def act(self, out, in_, func, bias=None, scale=1.0, accum_out=None):
        kw = {"scale": scale}
        rd = [in_]
        wr = [out]
        if bias is not None:
            kw["bias"] = bias
            if not isinstance(bias, (int, float)):
                rd.append(bias)
        if not isinstance(scale, (int, float)):
            rd.append(scale)
        if accum_out is not None:
            kw["accum_out"] = accum_out
            wr.append(accum_out)
        return self.add("act", lambda e: e.activation(out, in_, func, **kw), reads=rd, writes=wr, desc="act")

    def tt(self, eng, out, in0, in1, op):
        return self.add(eng, lambda e: e.tensor_tensor(out, in0, in1, op), reads=[in0, in1], writes=[out], desc="tt")

    def ts(self, eng, out, in0, s1, s2, op0, op1=None):
        rd = [in0] + [s for s in (s1, s2) if s is not None and not isinstance(s, (int, float))]
        if op1 is None:
            return self.add(eng, lambda e: e.tensor_scalar(out, in0, s1, None, op0), reads=rd, writes=[out], desc="ts")
        return self.add(eng, lambda e: e.tensor_scalar(out, in0, s1, s2, op0, op1), reads=rd, writes=[out], desc="ts")

    def stt(self, eng, out, in0, scalar, in1, op0, op1):
        rd = [in0, in1] + ([scalar] if not isinstance(scalar, (int, float)) else [])
        return self.add(eng, lambda e: e.scalar_tensor_tensor(out, in0, scalar, in1, op0, op1),
                        reads=rd, writes=[out], desc="stt")

    def stt_acc(self, eng, out, in0, scalar, in1, op0, op1, accum_out):
        rd = [in0, in1] + ([scalar] if not isinstance(scalar, (int, float)) else [])
        return self.add(eng, lambda e: e.scalar_tensor_tensor(out, in0, scalar, in1, op0, op1, accum_out=accum_out),
                        reads=rd, writes=[out, accum_out], desc="stta")

    def copy(self, eng, out, in_):
        if eng == "act":
            return self.add(eng, lambda e: e.copy(out, in_), reads=[in_], writes=[out], desc="copy")
        return self.add(eng, lambda e: e.tensor_copy(out, in_), reads=[in_], writes=[out], desc="copy")

    def memset(self, eng, out, val):
        return self.add(eng, lambda e: e.memset(out, val), writes=[out], desc="memset")

    def recip(self, out, in_):
        return self.add("dve", lambda e: e.reciprocal(out, in_), reads=[in_], writes=[out], desc="recip")

    def dma(self, eng, out, in_, is_output=False):
        op = self.add(eng, lambda e: e.dma_start(out, in_), reads=[in_], writes=[out], is_dma=True, desc="dma")
        if is_output:
            self.out_dma_ops.append(op)
        return op


class Arena:
    def __init__(self, arena_ap, nbytes):
        self.a = arena_ap
        self.nbytes = nbytes
        self.top = 0

    def alloc(self, shape, dtype, at=None):
        es = _esize(dtype)
        n = 1
        for s in shape:
            n *= s
        nb = (n * es + 63) // 64 * 64
        if at is None:
            at = self.top
            self.top += nb
        assert at + nb <= self.nbytes, ("SBUF arena overflow", at, nb, self.nbytes)
        v = self.a[:, at // 4: (at + nb) // 4]
        if dtype != F32:
            v = v.bitcast(dtype)
        v = v[:, 0:n]
        if len(shape) == 2:
            v = v.rearrange("p (a b) -> p a b", a=shape[0])
        elif len(shape) == 3:
            v = v.rearrange("p (a b c) -> p a b c", a=shape[0], b=shape[1])
        return v, at, nb


def build_program(n_pass, debug=False, n_pre=0):
    nc = bass.Bass("TRN2", target_bir_lowering=False)
    LT = n_pass * TP
    LA = max(n_pre, 1) * TP

    def din(name, shape):
        return nc.dram_tensor(name, list(shape), F32, kind="ExternalInput").ap()

    xT = din("xT", [D, LT])
    xTa = din("xTa", [D, LA])
    role_d = din("role", [128, 1])
    c_bc = din("c_bc", [128, D])
    w_adaT = din("w_adaT", [128, 72, D])
    b_ada = din("b_ada", [128, 72])
    nw_d = din("nw", [128, 32])
    wg_d = [din("wg1", [D, DFF]), din("wg2", [D, DFF])]
    wu_d = [din("wu1", [D, DFF]), din("wu2", [D, DFF])]
    wd_d = [din("wd1", [DFF, D]), din("wd2", [DFF, D])]
    w_in = din("w_in", [D, DIN])
    cw_d = din("conv_w", [128, 64])
    cb_d = din("conv_b", [128, 16])
    hp_d = din("headp", [128, 48])
    snw_d = din("ssd_norm_w", [128, D])
    pw_d = din("pool_w", [4, 256, 256])
    pb_d = din("pool_b", [128, 8])
    psc_d = din("pool_scale", [128, 8])
    w_out = din("w_out", [2 * D, D])
    cf_d = din("constf", [128, 3 * 128])
    mask_d = din("maskT", [128, 4 * 128])
    pm_d = din("poolM", [128, 12 * 128])
    outT = nc.dram_tensor("outT", [D, LT], F32, kind="ExternalOutput").ap()
    dbg = []
    if debug:
        dbg = [nc.dram_tensor("dbg%d" % i, [D, TP], F32, kind="ExternalOutput").ap() for i in range(3)]

    xT_v = xT.rearrange("(k p) t -> p k t", p=128)
    xTa_v = xTa.rearrange("(k p) t -> p k t", p=128)
    outT_v = outT.rearrange("(k p) t -> p k t", p=128)
    w_in_v = w_in.rearrange("(k p) f -> p k f", p=128)
    w_out_v = w_out.rearrange("(k p) f -> p k f", p=128)

    with ExitStack() as st:
        E = st.enter_context
        ARENA_BYTES = 206 * 1024
        arena_t = E(nc.sbuf_tensor("arena", [128, ARENA_BYTES // 4], F32))
        A = Arena(arena_t[:], ARENA_BYTES)
        ps = [E(nc.psum_tensor("ps%d" % i, [128, 512], F32)) for i in range(8)]
        psf = [p[:] for p in ps]
        psb = [p[:].bitcast(BF16) for p in ps]
        P = Prog(nc)

        def al(shape, dtype):
            return A.alloc(shape, dtype)[0]

        constf = al([3, 128], F32)
        ident_f, tri_f, ones_f = constf[:, 0, :], constf[:, 1, :], constf[:, 2, :]
        ident_b = al([128], BF16)
        ones_b = al([128], BF16)
        mask_b = al([4 * 128], BF16)
        poolM = al([12, 128], BF16)
        diag4 = [al([4, 128], BF16) for _ in range(2)]
        nw = al([4, 8], F32)
        mod = al([72], F32)
        Amod = al([3, 8], F32)
        gate = al([3, 8], F32)
        cw = al([16, 4], F32)
        cb = al([16], F32)
        headp = al([48], F32)
        aneg8 = al([NCH * 16], F32)
        dtb8 = al([NCH * 16], F32)
        snw = al([D], F32)
        poolw = al([4, 2, 256], BF16)
        pb = al([8], F32)
        psc = al([8], F32)
        pbs = al([8], F32)
        wdt = al([8, 16], BF16)
        epsc = al([1], F32)
        onec = al([1], F32)
        role = al([1], F32)
        prev = al([D], F32)
        prev_b2 = [al([D], BF16) for _ in range(2)]
        prev_b = prev_b2[0]
        uhalo = al([D], BF16)
        xhalo = al([16, 3], BF16)
        dtall = al([NCH * 16], F32)
        adt = al([NCH * 16], F32)
        acs = al([NCH * 16], F32)
        nacs = al([NCH * 16], F32)
        acs_hb = al([NCH * 16], BF16)
        sd = al([NCH * 16], F32)
        dtds = al([NCH * 16], F32)
        cdb = al([NCH * 16], F32)
        sp0 = al([NCH * 16], F32)
        sp1 = al([NCH * 16], F32)
        ssq = al([4], F32)
        grs = al([4], F32)
        xres = al([8, TP], F32)
        sqb = [al([512], BF16) for _ in range(2)]
        rstd2 = [al([512], F32) for _ in range(2)]
        ntmp = [al([512], F32) for _ in range(2)]
        PH = A.top

        A.top = PH
        hT = al([8, TP], BF16)
        actb = al([NF, TP], BF16)
        wgu = [al([2, 8, 256], BF16) for _ in range(2)]
        wdn = [al([NF, 256], BF16) for _ in range(2)]
        sgt = [al([512], BF16) for _ in range(2)]
        _keep = A.top
        A.top = PH
        ostage2 = [al([8, 512], F32), al([8, 512], F32)]
        A.top = _keep
        cact = al([D], F32)
        junk = al([D], F32)
        bada = al([72], F32)
        modacc = al([72], F32)
        aexp = al([16], F32)
        wsm = [al([D], F32) for _ in range(4)]
        ffn_top = A.top
        A.top = PH
        wada = [al([8, D], F32) for _ in range(2)]
        setup_top = A.top
        A.top = PH
        r1_at = A.top
        xbc_raw = al([16, 3 + TP], BF16)
        A.top = r1_at
        ycatT = al([16, TP], BF16)
        A.top = r1_at + (16 * (3 + TP) * 2 + 63) // 64 * 64
        xbc_c = al([16, TP], BF16)
        z_tok = al([NCH, D], BF16)
        X_at = A.top
        h2T = al([8, TP], BF16)
        u_tok = al([NCH + 1, D], BF16)
        wst = [al([8, 512], BF16) for _ in range(2)]
        inproj_top = A.top
        A.top = X_at + 16 * 1024 + (NCH + 1) * D * 2
        diffT = al([8, 512], BF16)
        A.top = X_at
        xdt = [al([D], BF16) for _ in range(2)]
        xdtds = [al([D], BF16) for _ in range(2)]
        xsD = [al([D], BF16) for _ in range(2)]
        Btok = [al([512], BF16) for _ in range(2)]
        Rhi = al([16, 128], BF16)
        Rlo = al([16, 128], BF16)
        decT = [al([16, 128], BF16) for _ in range(2)]
        scT = [al([16, 128], BF16) for _ in range(2)]
        t1 = al([D], F32)
        szb = al([D], F32)
        ynb = al([D], BF16)
        chunk_top = A.top
        A.top = X_at
        wo = [al([16, 256], BF16) for _ in range(2)]
        mix_top = max(inproj_top, chunk_top, A.top)
        assert max(ffn_top, setup_top, mix_top) <= ARENA_BYTES, (ffn_top, setup_top, mix_top)

        rot = {"a": 0}

        def alt():
            rot["a"] ^= 1
            return "act" if rot["a"] else "dve"

        def bc_last(ap2, n):
            return ap2.unsqueeze(2).to_broadcast([128, ap2.shape[1], n])

        def bc_mid(ap2, n):
            return ap2.unsqueeze(1).to_broadcast([128, n, ap2.shape[1]])

        P.dma("sp", wada[0], w_adaT[:, 0:8, :])
        P.dma("pool", wada[1], w_adaT[:, 8:16, :])
        P.dma("sp", constf, cf_d.rearrange("p (a b) -> p a b", a=3))
        P.dma("pool", mask_b, mask_d)
        P.dma("pool", poolM, pm_d.rearrange("p (a b) -> p a b", a=12))
        P.dma("sp", nw, nw_d.rearrange("p (a b) -> p a b", a=4))
        P.dma("sp", cw, cw_d.rearrange("p (a b) -> p a b", a=16))
        P.dma("sp", cb, cb_d)
        P.dma("sp", headp, hp_d)
        P.dma("sp", snw, snw_d)
        P.dma("sp", pb, pb_d)
        P.dma("sp", psc, psc_d)
        P.dma("sp", bada, b_ada)
        P.dma("sp", role, role_d)
        P.dma("sp", cact, c_bc)
        P.dma("pool", poolw, pw_d.rearrange("g (k c) d -> c g k d", c=128))
        P.dma("pool", wdt, w_in_v[:, :, 3072:3088])
        P.copy("dve", ident_b, ident_f)
        P.copy("dve", ones_b, ones_f)
        P.memset("dve", epsc, EPS)
        P.memset("dve", onec, 1.0)
        P.memset("dve", prev, 0.0)
        P.memset("dve", prev_b, 0.0)
        P.memset("dve", uhalo, 0.0)
        P.memset("dve", xhalo, 0.0)
        P.act(cact, cact, AF.Silu)
        def mod_j(j, wrow):
            P.stt_acc("dve", junk, wrow, 1.0, cact, ALU.mult, ALU.mult, modacc[:, j:j + 1])

        def mod_vec_done(v):
            vs = slice(v * 8, (v + 1) * 8)
            P.tt("dve", mod[:, vs], modacc[:, vs], bada[:, vs], ALU.add)
            i = v // 3
            if v % 3 == 1:
                P.stt("dve", Amod[:, i, :], mod[:, vs], 1.0, nw[:, i, :], ALU.add, ALU.mult)
            elif v % 3 == 2:
                P.ts("dve", gate[:, i, :], mod[:, vs], 1.0 if i == 1 else 0.5, None, ALU.mult)

        for ch in range(2):
            wb = wada[ch % 2]
            for jj in range(8):
                mod_j(ch * 8 + jj, wb[:, jj, :])
            mod_vec_done(ch)

        def bg_mod():
            for j in range(16, 72):
                bg["j"] = j + 1
                wb = wsm[j % 4]
                P.dma("sp", wb, w_adaT[:, j, :])
                mod_j(j, wb)
                if j % 8 == 7:
                    mod_vec_done(j // 8)
                yield

        bg = {"gen": None, "j": 16}
        bg["gen"] = bg_mod()

        def bg_step(n):
            for _ in range(n):
                if bg["gen"] is None:
                    return
                try:
                    next(bg["gen"])
                except StopIteration:
                    bg["gen"] = None
        P.act(aexp, headp[:, 16:32], AF.Exp)
        for c in range(NCH):
            P.ts("dve", aneg8[:, c * 16:(c + 1) * 16], aexp, -1.0, None, ALU.mult)
            P.copy("dve", dtb8[:, c * 16:(c + 1) * 16], headp[:, 0:16])
        P.tt("dve", pbs, pb, psc, ALU.mult)

        sqc = {"n": 0, "pend": None}

        def sumsq_rstd(blk, have_sums):
            bs = slice(blk * 512, (blk + 1) * 512)
            if not have_sums:
                for k in range(8):
                    P.act(sqb[k % 2], xres[:, k, bs], AF.Square)
                    P.mm(psf[6 + blk], ones_b, sqb[k % 2], start=(k == 0), stop=(k == 7))
            P.act(rstd2[blk], psf[6 + blk], AF.Sqrt, bias=epsc[:, 0:1], scale=1.0 / D)
            P.recip(rstd2[blk], rstd2[blk])

        def res_sumsq_flush():
            if sqc["pend"] is not None:
                blk, buf, first, last = sqc["pend"]
                P.mm(psf[6 + blk], ones_b, buf, start=first, stop=last)
                sqc["pend"] = None

        def res_sumsq(dtile, blk):
            res_sumsq_flush()
            buf = sqb[sqc["n"] % 2]
            sqc["n"] += 1
            P.act(buf, xres[:, dtile, blk * 512:(blk + 1) * 512], AF.Square)
            sqc["pend"] = (blk, buf, dtile == 0, dtile == 7)

        def norm_mod(i, dst, have_sums=False):
            for blk in range(NBLK):
                sumsq_rstd(blk, have_sums)
            for blk in range(NBLK):
                bs = slice(blk * 512, (blk + 1) * 512)
                for k in range(8):
                    P.stt("dve", ntmp[k % 2], xres[:, k, bs], Amod[:, i, k:k + 1], rstd2[blk], ALU.mult, ALU.mult)
                    P.act(dst[:, k, bs], ntmp[k % 2], AF.Identity, bias=mod[:, 3 * i * 8 + k:3 * i * 8 + k + 1], scale=1.0)

        cnt = {"g": 0, "d": 0, "x": 0, "c": 0}

        def ffn(fi):
            wg_v = wg_d[fi].rearrange("(k p) f -> p k f", p=128)
            wu_v = wu_d[fi].rearrange("(k p) f -> p k f", p=128)
            wd_v = wd_d[fi].rearrange("(f p) d -> p f d", p=128)
            i = 2 * fi
            norm_mod(i, hT, have_sums=(fi == 1))
            for pr in range(NF // 2):
                slot = wgu[pr % 2]
                P.dma("pool", slot[:, 0], wg_v[:, :, pr * 256:(pr + 1) * 256])
                P.dma("pool", slot[:, 1], wu_v[:, :, pr * 256:(pr + 1) * 256])
                for sub in range(2):
                    j = pr * 2 + sub
                    for blk in range(NBLK):
                        bs = slice(blk * 512, (blk + 1) * 512)
                        x = cnt["g"] % 2
                        cnt["g"] += 1
                        pg, pu = psf[x], psf[2 + x]
                        for k in range(8):
                            P.mm(pg, slot[:, 0, k, sub * 128:(sub + 1) * 128], hT[:, k, bs], start=(k == 0), stop=(k == 7))
                        for k in range(8):
                            P.mm(pu, slot[:, 1, k, sub * 128:(sub + 1) * 128], hT[:, k, bs], start=(k == 0), stop=(k == 7))
                        P.act(sgt[x], pg, AF.Silu)
                        P.tt("dve", actb[:, j, bs], sgt[x], pu, ALU.mult)
                        bg_step(2 if (pr * 4 + sub * 2 + blk) < 4 else 1)
            if bg["gen"] is not None:
                for _ in range(8):
                    if bg["j"] < 24:
                        bg_step(1)
            for pr in range(4):
                slot = wdn[pr % 2]
                P.dma("pool", slot, wd_v[:, :, pr * 256:(pr + 1) * 256])
                for sub in range(2):
                    dtile = pr * 2 + sub
                    for blk in range(NBLK):
                        bs = slice(blk * 512, (blk + 1) * 512)
                        pd = psf[4 + cnt["d"] % 2]
                        cnt["d"] += 1
                        for f in range(NF):
                            P.mm(pd, slot[:, f, sub * 128:(sub + 1) * 128], actb[:, f, bs], start=(f == 0), stop=(f == NF - 1))
                        P.stt("dve", xres[:, dtile, bs], pd, gate[:, i, dtile:dtile + 1], xres[:, dtile, bs], ALU.mult, ALU.add)
                        res_sumsq(dtile, blk)
                        bg_step(1)
            res_sumsq_flush()
            bg_step(100)

        def nextps4():
            x = cnt["x"] % 4
            cnt["x"] += 1
            return x

        def mixer(pi, a1=False, last_pre=False):
            norm_mod(1, h2T, have_sums=True)
            P.copy("dve", xbc_raw[:, :, 0:3], xhalo)
            P.copy("dve", u_tok[:, 0, :], uhalo)
            def dt_block():
                for c in range(NCH):
                    cs = slice(c * 128, (c + 1) * 128)
                    for k in range(8):
                        P.mm(psf[4][:, c * 16:(c + 1) * 16], h2T[:, k, cs], wdt[:, k, :], start=(k == 0), stop=(k == 7))
                NH = NCH * 16
                P.tt("dve", sp0, psf[4][:, 0:NH], dtb8, ALU.add)
                P.ts("dve", sp1, sp0, -1.0, None, ALU.mult)
                P.tt("dve", sp1, sp1, sp0, ALU.min)
                P.act(sp1, sp1, AF.Exp)
                P.act(sp1, sp1, AF.Ln, bias=onec[:, 0:1], scale=1.0)
                P.ts("dve", sp0, sp0, 0.0, None, ALU.max)
                P.tt("dve", dtall, sp0, sp1, ALU.add)
                P.tt("dve", adt, dtall, aneg8, ALU.mult)
                P.mm(psf[5][:, 0:NH], tri_f, adt)
                P.mm(psf[5][:, NH:2 * NH], ones_f, adt)
                P.copy("dve", acs, psf[5][:, 0:NH])
                P.ts("dve", nacs, acs, -1.0, None, ALU.mult)
                P.act(sd, acs, AF.Exp)
                P.tt("dve", sp0, psf[5][:, NH:2 * NH], acs, ALU.subtract)
                P.act(sp0, sp0, AF.Exp)
                P.tt("dve", dtds, dtall, sp0, ALU.mult)
                P.act(cdb, psf[5][:, NH:2 * NH], AF.Exp)
                P.copy("dve", acs_hb, acs)
                P.copy("dve", sp1, acs_hb)
                P.tt("dve", sp0, acs, sp1, ALU.subtract)

            for pr in range(8):
                if pr == 3:
                    dt_block()
                if a1 and pr >= 6 and not last_pre:
                    continue
                slot = wst[pr % 2]
                P.dma("pool", slot[:, :, 0:256], w_in_v[:, :, 1024 + pr * 256:1024 + (pr + 1) * 256])
                for sub in range(2):
                    i = pr * 2 + sub
                    for blk in range(NBLK):
                        if a1 and pr >= 6 and blk != NBLK - 1:
                            continue
                        bs = slice(blk * 512, (blk + 1) * 512)
                        pp = psf[nextps4()]
                        for k in range(8):
                            P.mm(pp, slot[:, k, sub * 128:(sub + 1) * 128], h2T[:, k, bs], start=(k == 0), stop=(k == 7))
                        P.copy(alt(), xbc_raw[:, i, 3 + blk * 512:3 + (blk + 1) * 512], pp)
            for grp in range(4):
                if a1 and (grp < 2 or not last_pre):
                    continue
                slot = wst[grp % 2]
                col0 = (0, 512, 3088, 3600)[grp]
                P.dma("pool", slot, w_in_v[:, :, col0:col0 + 512])
                for c in (range(NCH - 1, NCH) if a1 else range(NCH)):
                    cs = slice(c * 128, (c + 1) * 128)
                    pp = psf[nextps4()]
                    for k in range(8):
                        P.mm(pp, h2T[:, k, cs], slot[:, k, :], start=(k == 0), stop=(k == 7))
                    if grp < 2:
                        P.act(z_tok[:, c, grp * 512:(grp + 1) * 512], pp, AF.Silu)
                    else:
                        P.copy(alt(), u_tok[:, c + 1, (grp - 2) * 512:(grp - 1) * 512], pp)
            for i in range(12 if a1 else 16):
                if i % 3 == 1:
                    for blk in range(NBLK):
                        acc = ntmp[blk % 2]
                        P.ts("dve", acc, xbc_raw[:, i, blk * 512:blk * 512 + 512], cw[:, i, 0:1], None, ALU.mult)
                        for k in range(1, 4):
                            P.stt("dve", acc, xbc_raw[:, i, blk * 512 + k:blk * 512 + k + 512], cw[:, i, k:k + 1], acc, ALU.mult, ALU.add)
                        P.act(xbc_c[:, i, blk * 512:(blk + 1) * 512], acc, AF.Silu, bias=cb[:, i:i + 1], scale=1.0)
                    continue
                dg = diag4[cnt["c"] % 2]
                cnt["c"] += 1
                for k in range(4):
                    P.ts("dve", dg[:, k, :], ident_f, cw[:, i, k:k + 1], None, ALU.mult)
                for blk in range(NBLK):
                    pp = psf[nextps4()]
                    for k in range(4):
                        P.mm(pp, dg[:, k, :], xbc_raw[:, i, blk * 512 + k:blk * 512 + k + 512], start=(k == 0), stop=(k == 3))
                    P.act(xbc_c[:, i, blk * 512:(blk + 1) * 512], pp, AF.Silu, bias=cb[:, i:i + 1], scale=1.0)
            P.copy("dve", xhalo, xbc_raw[:, :, TP:TP + 3])
            for blk in range(0 if a1 else NBLK):
                bs = slice(blk * 512, (blk + 1) * 512)
                for i in range(8):
                    g = i // 2
                    pp = psf[i % 2]
                    for cc in range(4):
                        c = blk * 4 + cc
                        first = (pi == 0 and c == 0)
                        Mx = poolM[:, g, :] if first else poolM[:, 4 + g, :]
                        P.mm(pp[:, cc * 128:(cc + 1) * 128], u_tok[:, c + 1, i * 128:(i + 1) * 128], Mx, start=True, stop=False)
                        P.mm(pp[:, cc * 128:(cc + 1) * 128], u_tok[:, c, i * 128:(i + 1) * 128], poolM[:, 8 + g, :], start=False, stop=True)
                    P.copy(alt(), diffT[:, i, :], pp)
                for ot in range(8):
                    g, kd = ot // 2, ot % 2
                    pp = psf[2 + ot % 2]
                    for kc in range(2):
                        P.mm(pp, poolw[:, g, kc, kd * 128:(kd + 1) * 128], diffT[:, g * 2 + kc, :], start=(kc == 0), stop=(kc == 1))
                    P.act(ycatT[:, 8 + ot, bs], pp, AF.Identity, bias=pbs[:, ot:ot + 1], scale=psc[:, ot:ot + 1])
            if (not a1) or last_pre:
                P.copy("dve", uhalo, u_tok[:, NCH, :])
            def a1_front(c):
                cs = slice(c * 128, (c + 1) * 128)
                hs = slice(c * 16, (c + 1) * 16)
                par = c % 2
                tb0, tb1 = (0, 1) if par == 0 else (2, 4)
                for i in range(8):
                    P.transpose(psb[tb0][:, i * 128:(i + 1) * 128], xbc_c[:, i, cs], ident_b)
                for i in range(4):
                    P.transpose(psb[tb1][:, i * 128:(i + 1) * 128], xbc_c[:, 8 + i, cs], ident_b)
                P.tt("dve", xdtds[par].rearrange("p (h q) -> p h q", h=16), psb[tb0].rearrange("p (h q) -> p h q", h=16),
                     bc_last(dtds[:, hs], 64), ALU.mult)
                P.copy("act", Btok[par], psb[tb1][:, 0:512])

            def a1_back(c):
                hs = slice(c * 16, (c + 1) * 16)
                par = c % 2
                sb0, sb1 = (7, 3) if par == 0 else (5, 6)
                for g in range(4):
                    bank = sb0 if g < 2 else sb1
                    P.mm(psf[bank][:, (g % 2) * 256:(g % 2 + 1) * 256], Btok[par][:, g * 128:(g + 1) * 128],
                         xdtds[par][:, g * 256:(g + 1) * 256])
                P.tt("dve", prev.rearrange("p (h q) -> p h q", h=16), prev.rearrange("p (h q) -> p h q", h=16),
                     bc_last(cdb[:, hs], 64), ALU.mult)
                for b2 in range(2):
                    bank = sb0 if b2 == 0 else sb1
                    P.tt("dve", prev[:, b2 * 512:(b2 + 1) * 512], prev[:, b2 * 512:(b2 + 1) * 512], psf[bank], ALU.add)

            if a1:
                a1_front(0)
                for c in range(NCH):
                    if c + 1 < NCH:
                        a1_front(c + 1)
                    a1_back(c)
            def front(c):
                q = c % 2
                cs = slice(c * 128, (c + 1) * 128)
                hs = slice(c * 16, (c + 1) * 16)
                for i in range(8):
                    P.transpose(psb[0][:, i * 128:(i + 1) * 128], xbc_c[:, i, cs], ident_b)
                for i in range(4):
                    P.transpose(psb[1][:, i * 128:(i + 1) * 128], xbc_c[:, 8 + i, cs], ident_b)
                cb4 = psf[2].rearrange("p (g l) -> p g l", g=4)
                for g in range(4):
                    P.mm(cb4[:, g, :], xbc_c[:, 8 + g, cs], xbc_c[:, 12 + g, cs])
                P.tt("pool", Rhi, bc_mid(ident_f, 16), bc_last(sp1[:, hs], 128), ALU.mult)
                P.tt("pool", Rlo, bc_mid(ident_f, 16), bc_last(sp0[:, hs], 128), ALU.mult)
                yield
                xs3 = psb[0].rearrange("p (h q) -> p h q", h=16)
                P.tt("dve", xdt[q].rearrange("p (h q) -> p h q", h=16), xs3, bc_last(dtall[:, hs], 64), ALU.mult)
                P.tt("dve", xsD[q].rearrange("p (h q) -> p h q", h=16), xs3, bc_last(headp[:, 32:48], 64), ALU.mult)
                P.tt("dve", xdtds[q].rearrange("p (h q) -> p h q", h=16), xs3, bc_last(dtds[:, hs], 64), ALU.mult)
                P.copy("act", Btok[q], psb[1][:, 0:512])
                yield
                for g in range(4):
                    if g == 1:
                        P.tt("pool", prev.rearrange("p (h q) -> p h q", h=16), prev.rearrange("p (h q) -> p h q", h=16),
                             bc_last(cdb[:, hs], 64), ALU.mult)
                        for b2 in range(2):
                            for g_ in (2 * b2, 2 * b2 + 1):
                                P.mm(psf[7][:, (g_ % 2) * 256:(g_ % 2 + 1) * 256], Btok[q][:, g_ * 128:(g_ + 1) * 128],
                                     xdtds[q][:, g_ * 256:(g_ + 1) * 256])
                            P.tt("dve", prev[:, b2 * 512:(b2 + 1) * 512], prev[:, b2 * 512:(b2 + 1) * 512], psf[7], ALU.add)
                        P.copy("act", prev_b2[(c + 1) % 2], prev)
                        yield
                    sp_ = psf[g % 2]
                    P.mm(sp_, ones_b, Rhi[:, g * 4:(g + 1) * 4, :].rearrange("p a b -> p (a b)"), start=True, stop=False)
                    P.mm(sp_, ones_b, Rlo[:, g * 4:(g + 1) * 4, :].rearrange("p a b -> p (a b)"), start=False, stop=False)
                    P.mm(sp_, ident_b, mask_b, start=False, stop=True)
                    sp3 = sp_.rearrange("p (a b) -> p a b", a=4)
                    for r in range(4):
                        h = g * 4 + r
                        P.act(decT[q][:, h, :], sp3[:, r, :], AF.Exp, bias=nacs[:, c * 16 + h:c * 16 + h + 1], scale=1.0)
                    P.tt("dve", scT[q][:, g * 4:(g + 1) * 4, :], decT[q][:, g * 4:(g + 1) * 4, :], bc_mid(cb4[:, g, :], 4), ALU.mult)
                    yield

            def back(c):
                q = c % 2
                cs = slice(c * 128, (c + 1) * 128)
                hs = slice(c * 16, (c + 1) * 16)
                for g in range(4):
                    P.mm(psf[3 + g // 2][:, (g % 2) * 256:(g % 2 + 1) * 256], xbc_c[:, 12 + g, cs], prev_b2[c % 2][:, g * 256:(g + 1) * 256])
                for b2 in range(2):
                    P.mm(psf[5 + b2], ident_b, xsD[q][:, b2 * 512:(b2 + 1) * 512], start=True, stop=False)
                    for hh in range(8):
                        h = b2 * 8 + hh
                        P.mm(psf[5 + b2][:, hh * 64:(hh + 1) * 64], scT[q][:, h, :], xdt[q][:, h * 64:(h + 1) * 64], start=False, stop=(hh == 7))
                yield
                for b2 in range(2):
                    t13 = t1[:, b2 * 512:(b2 + 1) * 512].rearrange("p (h q) -> p h q", h=8)
                    P.tt("dve", t13, psf[3 + b2].rearrange("p (h q) -> p h q", h=8),
                         bc_last(sd[:, c * 16 + b2 * 8:c * 16 + b2 * 8 + 8], 64), ALU.mult)
                yield
                for b2 in range(2):
                    P.tt("dve", t1[:, b2 * 512:(b2 + 1) * 512], t1[:, b2 * 512:(b2 + 1) * 512], psf[5 + b2], ALU.add)
                yield
                P.tt("dve", t1, t1, z_tok[:, c, :], ALU.mult)
                for g in range(4):
                    gs = slice(g * 256, (g + 1) * 256)
                    P.stt_acc("dve", szb[:, gs], t1[:, gs], 1.0, t1[:, gs], ALU.mult, ALU.mult, ssq[:, g:g + 1])
                yield
                P.act(grs, ssq, AF.Ln, bias=epsc[:, 0:1], scale=1.0 / 256)
                P.act(grs, grs, AF.Exp, scale=-0.5)
                for g in range(4):
                    gs = slice(g * 256, (g + 1) * 256)
                    P.stt("dve", ynb[:, gs], t1[:, gs], grs[:, g:g + 1], snw[:, gs], ALU.mult, ALU.mult)
                yield
                for i in range(8):
                    P.transpose(psb[3][:, i * 128:(i + 1) * 128], ynb[:, i * 128:(i + 1) * 128], ident_b)
                P.copy("act", ycatT[:, 0:8, cs], psb[3].rearrange("p (a b) -> p a b", a=8))
                yield

            def drain(*gens):
                gens = list(gens)
                while gens:
                    for g_ in list(gens):
                        try:
                            next(g_)
                        except StopIteration:
                            gens.remove(g_)

            if not a1:
                P.copy("act", prev_b2[0], prev)
                drain(front(0))
                for c in range(NCH):
                    if c + 1 < NCH:
                        drain(front(c + 1), back(c))
                    else:
                        drain(back(c))
            for pr in range(0 if a1 else 4):
                slot = wo[pr % 2]
                P.dma("pool", slot, w_out_v[:, :, pr * 256:(pr + 1) * 256])
                for sub in range(2):
                    dtile = pr * 2 + sub
                    for blk in range(NBLK):
                        bs = slice(blk * 512, (blk + 1) * 512)
                        pp = psf[4 + cnt["d"] % 2]
                        cnt["d"] += 1
                        for k in range(16):
                            P.mm(pp, slot[:, k, sub * 128:(sub + 1) * 128], ycatT[:, k, bs], start=(k == 0), stop=(k == 15))
                        P.stt("dve", xres[:, dtile, bs], pp, gate[:, 1, dtile:dtile + 1], xres[:, dtile, bs], ALU.mult, ALU.add)
                        res_sumsq(dtile, blk)
            res_sumsq_flush()

        def final(pi):
            for blk in range(NBLK):
                bs = slice(blk * 512, (blk + 1) * 512)
                sumsq_rstd(blk, True)
                for k in range(8):
                    P.stt("dve", ostage2[blk][:, k, :], xres[:, k, bs], nw[:, 3, k:k + 1], rstd2[blk], ALU.mult, ALU.mult)
                if pi + 1 < n_pass:
                    P.dma("sp", xres[:, :, bs], xT_v[:, :, (pi + 1) * TP + blk * 512:(pi + 1) * TP + (blk + 1) * 512])
                P.dma("sp", outT_v[:, :, pi * TP + blk * 512:pi * TP + (blk + 1) * 512], ostage2[blk], is_output=True)

        def dump(i):
            if debug:
                P.dma("sp", dbg[i].rearrange("(k p) t -> p k t", p=128), xres, is_output=True)

        for pa in range(n_pre):
            for blk in range(NBLK):
                P.dma("sp", xres[:, :, blk * 512:(blk + 1) * 512], xTa_v[:, :, pa * TP + blk * 512:pa * TP + (blk + 1) * 512])
            ffn(0)
            mixer(1, a1=True, last_pre=(pa == n_pre - 1))
        if n_pre:
            P.ts("dve", prev, prev, role[:, 0:1], None, ALU.mult)
            P.copy("act", prev_b, prev)
            P.ts("dve", xhalo, xhalo, role[:, 0:1], None, ALU.mult)
            P.ts("dve", uhalo, uhalo, role[:, 0:1], None, ALU.mult)
        for pi in range(n_pass):
            if pi == 0:
                for blk in range(NBLK):
                    P.dma("sp", xres[:, :, blk * 512:(blk + 1) * 512], xT_v[:, :, blk * 512:(blk + 1) * 512])
            ffn(0)
            if pi == 0:
                dump(0)
            mixer(pi)
            if pi == 0:
                dump(1)
            ffn(1)
            if pi == 0:
                dump(2)
            final(pi)
        P.emit(st)
    return nc, P.stats


def _consts(first_is_seq_start):
    ident = np.eye(128, dtype=np.float32)
    tri = np.triu(np.ones((128, 128), np.float32))
    ones = np.ones((128, 128), np.float32)
    constf = np.concatenate([ident, tri, ones], axis=1)
    s = np.arange(128)[:, None]
    l = np.arange(128)[None, :]
    maskT = np.where(l >= s, 0.0, NEG).astype(np.float32)
    maskT = np.tile(maskT, (1, 4))
    mats = []
    windows = (2, 4, 8, 16)
    t = np.arange(128)[None, :]
    for kind in ("first", "diag", "off"):
        for w in windows:
            if kind == "off":
                sp_ = np.arange(128)[:, None] - 128
                m = np.where((t - sp_ >= 0) & (t - sp_ <= w - 1), 1.0 / w, 0.0)
            else:
                if kind == "first" and first_is_seq_start:
                    cntv = np.minimum(t + 1, w).astype(np.float64)
                else:
                    cntv = np.full_like(t, w, dtype=np.float64)
                m = np.where((t - s >= 0) & (t - s <= w - 1), 1.0 / cntv, 0.0) - (s == t)
            mats.append(m.astype(np.float32))
    poolM = np.concatenate(mats, axis=1)
    return constf, maskT, poolM


_CACHE = {}


def _prep_inputs(b, t0, n_tok, seq_start, inp, n_pre_tok=0):
    f = np.float32
    c = np.ascontiguousarray
    constf, maskT, poolM = _consts(seq_start)
    nwv = np.stack([inp["ffn1_norm"][0], inp["mix_norm"][0], inp["ffn2_norm"][0], inp["final_norm"]])
    m = {
        "xT": c(inp["x"][b, t0:t0 + n_tok].T),
        "xTa": c(inp["x"][b, 0:t0].T) if t0 > 0 else np.zeros((D, max(n_pre_tok, TP)), f),
        "role": np.full((128, 1), 1.0 if t0 > 0 else 0.0, f),
        "c_bc": c(np.broadcast_to(inp["c"][b], (128, D))),
        "w_adaT": _CACHE["w_adaT"],
        "b_ada": c(inp["b_ada"][0].reshape(72, 128).T),
        "nw": c(nwv.reshape(4, 8, 128).transpose(2, 0, 1).reshape(128, 32)),
        "wg1": _CACHE["wg1"], "wu1": _CACHE["wu1"], "wd1": _CACHE["wd1"],
        "wg2": _CACHE["wg2"], "wu2": _CACHE["wu2"], "wd2": _CACHE["wd2"],
        "w_in": _CACHE["w_in"],
        "conv_w": c(inp["conv_w"][0].reshape(4, 16, 128).transpose(2, 1, 0).reshape(128, 64)),
        "conv_b": c(inp["conv_b"][0].reshape(16, 128).T),
        "headp": c(np.broadcast_to(np.concatenate([inp["dt_bias"][0], inp["a_log"][0], inp["d_skip"][0]]), (128, 48))),
        "ssd_norm_w": c(np.broadcast_to(inp["ssd_norm_w"][0], (128, D))),
        "pool_w": _CACHE["pool_w"],
        "pool_b": c(inp["pool_b"][0].reshape(8, 128).T),
        "pool_scale": c(inp["pool_scale"][0].reshape(8, 128).T),
        "w_out": _CACHE["w_out"],
        "constf": constf, "maskT": maskT, "poolM": poolM,
    }
    return {k: np.asarray(v, dtype=f) for k, v in m.items()}


def kernel(**inputs):
    inp = {k: np.asarray(v) for k, v in inputs.items()}
    c = np.ascontiguousarray
    _CACHE["w_adaT"] = c(inp["w_ada"][0].T.reshape(72, 128, D).transpose(1, 0, 2))
    for nm, key in (("wg1", "ffn1_w_gate"), ("wu1", "ffn1_w_up"), ("wd1", "ffn1_w_down"),
                    ("wg2", "ffn2_w_gate"), ("wu2", "ffn2_w_up"), ("wd2", "ffn2_w_down"),
                    ("w_in", "w_in"), ("pool_w", "pool_w"), ("w_out", "w_out")):
        _CACHE[nm] = c(inp[key][0])
    H = L // 2
    n_pass = H // TP
    nc, stats = build_program(n_pass, n_pre=n_pass)
    in_maps = [_prep_inputs(core // 2, (core % 2) * H, H, core % 2 == 0, inp, n_pre_tok=H) for core in range(8)]
    res = run_bass_kernel_spmd(nc, in_maps, core_ids=list(range(8)))
    out = np.empty((4, L, D), np.float32)
    for core in range(8):
        out[core // 2, (core % 2) * H:(core % 2 + 1) * H] = res.results[core]["outT"].T
    return out
```

```python
from contextlib import ExitStack

import numpy as np
import concourse.bass as bass
import concourse.mybir as mybir
from concourse.bass_utils import run_bass_kernel_spmd

F32 = mybir.dt.float32
BF16 = mybir.dt.bfloat16
AF = mybir.ActivationFunctionType
ALU = mybir.AluOpType

D = 1024
L = 4096
DFF = 2816
NF = DFF // 128
DIN = 4112
EPS = 1e-6
TP = 1024
NBLK = TP // 512
NCH = TP // 128
NEG = -30000.0

COMPUTE = ("pe", "act", "dve", "pool")
TRACKED_DRAM = set()


def _esize(dt):
    return 4 if dt in (F32, mybir.dt.int32, mybir.dt.uint32) else 2


def _region(ap):
    tn = type(ap.tensor).__name__
    if tn.startswith("DRam"):
        if ap.tensor.name in TRACKED_DRAM:
            return (ap.tensor.name, 0, 1, 0, 1)
        return None
    pat = ap.ap
    pstep, pcnt = pat[0]
    off = int(ap.offset)
    es = _esize(ap.dtype)
    if pstep == 0:
        p0, f0 = 0, off
    else:
        p0 = off // pstep
        f0 = off - p0 * pstep
    ext = 0
    for st, cn in pat[1:]:
        ext += abs(st) * (cn - 1)
    if tn.startswith("PSum"):
        return (ap.tensor.name, p0, p0 + pcnt, 0, 2048)
    return (ap.tensor.name, p0, p0 + pcnt, f0 * es, (f0 + ext + 1) * es)


class Op:
    __slots__ = ("eng", "fn", "deps", "marked", "value", "idx", "is_dma", "sem", "desc")

    def __init__(self, eng, fn, is_dma, desc):
        self.eng = eng
        self.fn = fn
        self.deps = set()
        self.marked = False
        self.value = None
        self.is_dma = is_dma
        self.sem = None
        self.desc = desc


class Prog:
    def __init__(self, nc, same_engine_sync=True, dma_ring=12):
        self.nc = nc
        self.ops = []
        self.same_engine_sync = same_engine_sync
        self.dma_ring = dma_ring
        self.track = {}
        self.out_dma_ops = []

    def add(self, eng, fn, reads=(), writes=(), is_dma=False, desc=""):
        op = Op(eng, fn, is_dma, desc)
        op.idx = len(self.ops)
        self.ops.append(op)
        for ap in reads:
            if ap is None or isinstance(ap, (int, float)):
                continue
            r = _region(ap)
            if r is not None:
                self._access(op, r, False)
        for ap in writes:
            r = _region(ap)
            if r is not None:
                self._access(op, r, True)
        return op

    def _access(self, op, r, is_write):
        name, p0, p1, f0, f1 = r
        lst = self.track.get(name, [])
        keep = []
        for rec in lst:
            q0, q1, g0, g1, prev, pw = rec
            if q1 <= p0 or p1 <= q0 or g1 <= f0 or f1 <= g0:
                keep.append(rec)
                continue
            if (is_write or pw) and prev is not op:
                op.deps.add(prev)
            covered = q0 >= p0 and q1 <= p1 and g0 >= f0 and g1 <= f1
            if is_write and covered:
                continue
            if (not is_write) and (not pw) and covered and prev.eng == op.eng and not prev.is_dma and not op.is_dma:
                continue
            keep.append(rec)
        keep.append((p0, p1, f0, f1, op, is_write))
        self.track[name] = keep

    def _skip(self, d, op):
        if d.eng == op.eng and not d.is_dma and not op.is_dma:
            if d.eng == "pe" or not self.same_engine_sync:
                return True
        return False

    def emit(self, stack):
        nc = self.nc
        engs = {"pe": nc.tensor, "act": nc.scalar, "dve": nc.vector, "pool": nc.gpsimd, "sp": nc.sync}
        for op in self.ops:
            for d in op.deps:
                if not self._skip(d, op):
                    d.marked = True
        sems = {e: stack.enter_context(nc.semaphore("s_" + e)) for e in COMPUTE}
        rings = {e: [stack.enter_context(nc.semaphore("d_%s%d" % (e, i))) for i in range(self.dma_ring)]
                 for e in ("sp", "pool")}
        cnt = {e: 0 for e in COMPUTE}
        dcnt = {e: 0 for e in rings}
        per_eng = {e: [] for e in engs}
        for op in self.ops:
            per_eng[op.eng].append(op)
            if op.is_dma:
                i = dcnt[op.eng]
                dcnt[op.eng] += 1
                op.sem = rings[op.eng][i % self.dma_ring]
                op.value = 16 * (i // self.dma_ring + 1)
                op.marked = True
            elif op.marked:
                cnt[op.eng] += 1
                op.sem = sems[op.eng]
                op.value = cnt[op.eng]
        self.stats = dict(cnt=cnt, dcnt=dcnt, nops=len(self.ops))
        block = stack.enter_context(nc.Block())

        def make(ename):
            ops = per_eng[ename]

            def body(e):
                waited = {}
                for op in ops:
                    need = {}
                    for d in op.deps:
                        if d.sem is None or self._skip(d, op):
                            continue
                        k = d.sem.num
                        if need.get(k, (None, 0))[1] < d.value:
                            need[k] = (d.sem, d.value)
                    if op.is_dma and op.value > 16:
                        k = op.sem.num
                        if need.get(k, (None, 0))[1] < op.value - 16:
                            need[k] = (op.sem, op.value - 16)
                    for k, (s, v) in need.items():
                        if waited.get(k, 0) >= v:
                            continue
                        e.wait_ge(s, v)
                        waited[k] = v
                    ins = op.fn(e)
                    if op.is_dma:
                        ins.then_inc(op.sem, 16)
                    elif op.marked:
                        ins.then_inc(op.sem, 1)
                if ename == "sp":
                    for op in self.out_dma_ops:
                        e.wait_ge(op.sem, op.value)
            return body

        block.sync(make("sp"))
        block.tensor(make("pe"))
        block.scalar(make("act"))
        block.vector(make("dve"))
        block.gpsimd(make("pool"))

    def mm(self, out, lhsT, rhs, start=True, stop=True):
        return self.add("pe", lambda e: e.matmul(out, lhsT, rhs, start=start, stop=stop),
                        reads=[lhsT, rhs], writes=[out], desc="mm")

    def transpose(self, out, in_, ident):
        return self.add("pe", lambda e: e.transpose(out, in_, ident), reads=[in_, ident], writes=[out], desc="tr")

    def act(self, out, in_, func, bias=None, scale=1.0, accum_out=None):
        kw = {"scale": scale}
        rd = [in_]
        wr = [out]
        if bias is not None:
            kw["bias"] = bias
            if not isinstance(bias, (int, float)):
                rd.append(bias)
        if not isinstance(scale, (int, float)):
            rd.append(scale)
        if accum_out is not None:
            kw["accum_out"] = accum_out
            wr.append(accum_out)
        return self.add("act", lambda e: e.activation(out, in_, func, **kw), reads=rd, writes=wr, desc="act")

    def tt(self, eng, out, in0, in1, op):
        return self.add(eng, lambda e: e.tensor_tensor(out, in0, in1, op), reads=[in0, in1], writes=[out], desc="tt")

    def ts(self, eng, out, in0, s1, s2, op0, op1=None):
        rd = [in0] + [s for s in (s1, s2) if s is not None and not isinstance(s, (int, float))]
        if op1 is None:
            return self.add(eng, lambda e: e.tensor_scalar(out, in0, s1, None, op0), reads=rd, writes=[out], desc="ts")
        return self.add(eng, lambda e: e.tensor_scalar(out, in0, s1, s2, op0, op1), reads=rd, writes=[out], desc="ts")

    def stt(self, eng, out, in0, scalar, in1, op0, op1):
        rd = [in0, in1] + ([scalar] if not isinstance(scalar, (int, float)) else [])
        return self.add(eng, lambda e: e.scalar_tensor_tensor(out, in0, scalar, in1, op0, op1),
                        reads=rd, writes=[out], desc="stt")

    def stt_acc(self, eng, out, in0, scalar, in1, op0, op1, accum_out):
        rd = [in0, in1] + ([scalar] if not isinstance(scalar, (int, float)) else [])
        return self.add(eng, lambda e: e.scalar_tensor_tensor(out, in0, scalar, in1, op0, op1, accum_out=accum_out),
                        reads=rd, writes=[out, accum_out], desc="stta")

    def copy(self, eng, out, in_):
        if eng == "act":
            return self.add(eng, lambda e: e.copy(out, in_), reads=[in_], writes=[out], desc="copy")
        return self.add(eng, lambda e: e.tensor_copy(out, in_), reads=[in_], writes=[out], desc="copy")

    def memset(self, eng, out, val):
        return self.add(eng, lambda e: e.memset(out, val), writes=[out], desc="memset")

    def recip(self, out, in_):
        return self.add("dve", lambda e: e.reciprocal(out, in_), reads=[in_], writes=[out], desc="recip")

    def dma(self, eng, out, in_, is_output=False):
        op = self.add(eng, lambda e: e.dma_start(out, in_), reads=[in_], writes=[out], is_dma=True, desc="dma")
        if is_output:
            self.out_dma_ops.append(op)
        return op


class Arena:
    def __init__(self, arena_ap, nbytes):
        self.a = arena_ap
        self.nbytes = nbytes
        self.top = 0

    def alloc(self, shape, dtype, at=None):
        es = _esize(dtype)
        n = 1
        for s in shape:
            n *= s
        nb = (n * es + 63) // 64 * 64
        if at is None:
            at = self.top
            self.top += nb
        assert at + nb <= self.nbytes, ("SBUF arena overflow", at, nb, self.nbytes)
        v = self.a[:, at // 4: (at + nb) // 4]
        if dtype != F32:
            v = v.bitcast(dtype)
        v = v[:, 0:n]
        if len(shape) == 2:
            v = v.rearrange("p (a b) -> p a b", a=shape[0])
        elif len(shape) == 3:
            v = v.rearrange("p (a b c) -> p a b c", a=shape[0], b=shape[1])
        return v, at, nb


def build_program(n_pass, debug=False, n_pre=0):
    nc = bass.Bass("TRN2", target_bir_lowering=False)
    LT = n_pass * TP
    LA = max(n_pre, 1) * TP

    def din(name, shape):
        return nc.dram_tensor(name, list(shape), F32, kind="ExternalInput").ap()

    xT = din("xT", [D, LT])
    xTa = din("xTa", [D, LA])
    role_d = din("role", [128, 1])
    c_bc = din("c_bc", [128, D])
    w_adaT = din("w_adaT", [128, 72, D])
    b_ada = din("b_ada", [128, 72])
    nw_d = din("nw", [128, 32])
    wg_d = [din("wg1", [D, DFF]), din("wg2", [D, DFF])]
    wu_d = [din("wu1", [D, DFF]), din("wu2", [D, DFF])]
    wd_d = [din("wd1", [DFF, D]), din("wd2", [DFF, D])]
    w_in = din("w_in", [D, DIN])
    cw_d = din("conv_w", [128, 64])
    cb_d = din("conv_b", [128, 16])
    hp_d = din("headp", [128, 48])
    snw_d = din("ssd_norm_w", [128, D])
    pw_d = din("pool_w", [4, 256, 256])
    pb_d = din("pool_b", [128, 8])
    psc_d = din("pool_scale", [128, 8])
    w_out = din("w_out", [2 * D, D])
    cf_d = din("constf", [128, 3 * 128])
    mask_d = din("maskT", [128, 4 * 128])
    pm_d = din("poolM", [128, 12 * 128])
    outT = nc.dram_tensor("outT", [D, LT], F32, kind="ExternalOutput").ap()
    dbg = []
    if debug:
        dbg = [nc.dram_tensor("dbg%d" % i, [D, TP], F32, kind="ExternalOutput").ap() for i in range(3)]

    xT_v = xT.rearrange("(k p) t -> p k t", p=128)
    xTa_v = xTa.rearrange("(k p) t -> p k t", p=128)
    outT_v = outT.rearrange("(k p) t -> p k t", p=128)
    w_in_v = w_in.rearrange("(k p) f -> p k f", p=128)
    w_out_v = w_out.rearrange("(k p) f -> p k f", p=128)

    with ExitStack() as st:
        E = st.enter_context
        ARENA_BYTES = 206 * 1024
        arena_t = E(nc.sbuf_tensor("arena", [128, ARENA_BYTES // 4], F32))
        A = Arena(arena_t[:], ARENA_BYTES)
        ps = [E(nc.psum_tensor("ps%d" % i, [128, 512], F32)) for i in range(8)]
        psf = [p[:] for p in ps]
        psb = [p[:].bitcast(BF16) for p in ps]
        P = Prog(nc)

        def al(shape, dtype):
            return A.alloc(shape, dtype)[0]

        constf = al([3, 128], F32)
        ident_f, tri_f, ones_f = constf[:, 0, :], constf[:, 1, :], constf[:, 2, :]
        ident_b = al([128], BF16)
        ones_b = al([128], BF16)
        mask_b = al([4 * 128], BF16)
        poolM = al([12, 128], BF16)
        diag4 = [al([4, 128], BF16) for _ in range(2)]
        nw = al([4, 8], F32)
        mod = al([72], F32)
        Amod = al([3, 8], F32)
        gate = al([3, 8], F32)
        cw = al([16, 4], F32)
        cb = al([16], F32)
        headp = al([48], F32)
        aneg8 = al([NCH * 16], F32)
        dtb8 = al([NCH * 16], F32)
        snw = al([D], F32)
        poolw = al([4, 2, 256], BF16)
        pb = al([8], F32)
        psc = al([8], F32)
        pbs = al([8], F32)
        wdt = al([8, 16], BF16)
        epsc = al([1], F32)
        onec = al([1], F32)
        role = al([1], F32)
        prev = al([D], F32)
        prev_b2 = [al([D], BF16) for _ in range(2)]
        prev_b = prev_b2[0]
        uhalo = al([D], BF16)
        xhalo = al([16, 3], BF16)
        dtall = al([NCH * 16], F32)
        adt = al([NCH * 16], F32)
        acs = al([NCH * 16], F32)
        nacs = al([NCH * 16], F32)
        acs_hb = al([NCH * 16], BF16)
        sd = al([NCH * 16], F32)
        dtds = al([NCH * 16], F32)
        cdb = al([NCH * 16], F32)
        sp0 = al([NCH * 16], F32)
        sp1 = al([NCH * 16], F32)
        ssq = al([4], F32)
        grs = al([4], F32)
        xres = al([8, TP], F32)
        sqb = [al([512], BF16) for _ in range(2)]
        rstd2 = [al([512], F32) for _ in range(2)]
        ntmp = [al([512], F32) for _ in range(2)]
        PH = A.top

        A.top = PH
        hT = al([8, TP], BF16)
        actb = al([NF, TP], BF16)
        wgu = [al([2, 8, 256], BF16) for _ in range(2)]
        wdn = [al([NF, 256], BF16) for _ in range(2)]
        sgt = [al([512], BF16) for _ in range(2)]
        _keep = A.top
        A.top = PH
        ostage2 = [al([8, 512], F32), al([8, 512], F32)]
        A.top = _keep
        cact = al([D], F32)
        junk = al([D], F32)
        bada = al([72], F32)
        modacc = al([72], F32)
        aexp = al([16], F32)
        wsm = [al([D], F32) for _ in range(4)]
        ffn_top = A.top
        A.top = PH
        wada = [al([8, D], F32) for _ in range(2)]
        setup_top = A.top
        A.top = PH
        r1_at = A.top
        xbc_raw = al([16, 3 + TP], BF16)
        A.top = r1_at
        ycatT = al([16, TP], BF16)
        A.top = r1_at + (16 * (3 + TP) * 2 + 63) // 64 * 64
        xbc_c = al([16, TP], BF16)
        z_tok = al([NCH, D], BF16)
        X_at = A.top
        h2T = al([8, TP], BF16)
        u_tok = al([NCH + 1, D], BF16)
        wst = [al([8, 512], BF16) for _ in range(2)]
        inproj_top = A.top
        A.top = X_at + 16 * 1024 + (NCH + 1) * D * 2
        diffT = al([8, 512], BF16)
        A.top = X_at
        xdt = [al([D], BF16) for _ in range(2)]
        xdtds = [al([D], BF16) for _ in range(2)]
        xsD = [al([D], BF16) for _ in range(2)]
        Btok = [al([512], BF16) for _ in range(2)]
        Rhi = al([16, 128], BF16)
        Rlo = al([16, 128], BF16)
        decT = [al([16, 128], BF16) for _ in range(2)]
        scT = [al([16, 128], BF16) for _ in range(2)]
        t1 = al([D], F32)
        szb = al([D], F32)
        ynb = al([D], BF16)
        chunk_top = A.top
        A.top = X_at
        wo = [al([16, 256], BF16) for _ in range(2)]
        mix_top = max(inproj_top, chunk_top, A.top)
        assert max(ffn_top, setup_top, mix_top) <= ARENA_BYTES, (ffn_top, setup_top, mix_top)

        rot = {"a": 0}

        def alt():
            rot["a"] ^= 1
            return "act" if rot["a"] else "dve"

        def bc_last(ap2, n):
            return ap2.unsqueeze(2).to_broadcast([128, ap2.shape[1], n])

        def bc_mid(ap2, n):
            return ap2.unsqueeze(1).to_broadcast([128, n, ap2.shape[1]])

        P.dma("sp", wada[0], w_adaT[:, 0:8, :])
        P.dma("pool", wada[1], w_adaT[:, 8:16, :])
        P.dma("sp", constf, cf_d.rearrange("p (a b) -> p a b", a=3))
        P.dma("pool", mask_b, mask_d)
        P.dma("pool", poolM, pm_d.rearrange("p (a b) -> p a b", a=12))
        P.dma("sp", nw, nw_d.rearrange("p (a b) -> p a b", a=4))
        P.dma("sp", cw, cw_d.rearrange("p (a b) -> p a b", a=16))
        P.dma("sp", cb, cb_d)
        P.dma("sp", headp, hp_d)
        P.dma("sp", snw, snw_d)
        P.dma("sp", pb, pb_d)
        P.dma("sp", psc, psc_d)
        P.dma("sp", bada, b_ada)
        P.dma("sp", role, role_d)
        P.dma("sp", cact, c_bc)
        P.dma("pool", poolw, pw_d.rearrange("g (k c) d -> c g k d", c=128))
        P.dma("pool", wdt, w_in_v[:, :, 3072:3088])
        P.copy("dve", ident_b, ident_f)
        P.copy("dve", ones_b, ones_f)
        P.memset("dve", epsc, EPS)
        P.memset("dve", onec, 1.0)
        P.memset("dve", prev, 0.0)
        P.memset("dve", prev_b, 0.0)
        P.memset("dve", uhalo, 0.0)
        P.memset("dve", xhalo, 0.0)
        P.act(cact, cact, AF.Silu)
        def mod_j(j, wrow):
            P.stt_acc("dve", junk, wrow, 1.0, cact, ALU.mult, ALU.mult, modacc[:, j:j + 1])

        def mod_vec_done(v):
            vs = slice(v * 8, (v + 1) * 8)
            P.tt("dve", mod[:, vs], modacc[:, vs], bada[:, vs], ALU.add)
            i = v // 3
            if v % 3 == 1:
                P.stt("dve", Amod[:, i, :], mod[:, vs], 1.0, nw[:, i, :], ALU.add, ALU.mult)
            elif v % 3 == 2:
                P.ts("dve", gate[:, i, :], mod[:, vs], 1.0 if i == 1 else 0.5, None, ALU.mult)

        for ch in range(2):
            wb = wada[ch % 2]
            for jj in range(8):
                mod_j(ch * 8 + jj, wb[:, jj, :])
            mod_vec_done(ch)

        def bg_mod():
            for j in range(16, 72):
                bg["j"] = j + 1
                wb = wsm[j % 4]
                P.dma("sp", wb, w_adaT[:, j, :])
                mod_j(j, wb)
                if j % 8 == 7:
                    mod_vec_done(j // 8)
                yield

        bg = {"gen": None, "j": 16}
        bg["gen"] = bg_mod()

        def bg_step(n):
            for _ in range(n):
                if bg["gen"] is None:
                    return
                try:
                    next(bg["gen"])
                except StopIteration:
                    bg["gen"] = None
        P.act(aexp, headp[:, 16:32], AF.Exp)
        for c in range(NCH):
            P.ts("dve", aneg8[:, c * 16:(c + 1) * 16], aexp, -1.0, None, ALU.mult)
            P.copy("dve", dtb8[:, c * 16:(c + 1) * 16], headp[:, 0:16])
        P.tt("dve", pbs, pb, psc, ALU.mult)

        sqc = {"n": 0, "pend": None}

        def sumsq_rstd(blk, have_sums):
            bs = slice(blk * 512, (blk + 1) * 512)
            if not have_sums:
                for k in range(8):
                    P.act(sqb[k % 2], xres[:, k, bs], AF.Square)
                    P.mm(psf[6 + blk], ones_b, sqb[k % 2], start=(k == 0), stop=(k == 7))
            P.act(rstd2[blk], psf[6 + blk], AF.Ln, bias=epsc[:, 0:1], scale=1.0 / D)
            P.act(rstd2[blk], rstd2[blk], AF.Exp, scale=-0.5)

        def res_sumsq_flush():
            if sqc["pend"] is not None:
                blk, buf, first, last = sqc["pend"]
                P.mm(psf[6 + blk], ones_b, buf, start=first, stop=last)
                sqc["pend"] = None

        def res_sumsq(dtile, blk):
            res_sumsq_flush()
            buf = sqb[sqc["n"] % 2]
            sqc["n"] += 1
            P.act(buf, xres[:, dtile, blk * 512:(blk + 1) * 512], AF.Square)
            sqc["pend"] = (blk, buf, dtile == 0, dtile == 7)

        def norm_mod(i, dst, have_sums=False):
            for blk in range(NBLK):
                sumsq_rstd(blk, have_sums)
            for blk in range(NBLK):
                bs = slice(blk * 512, (blk + 1) * 512)
                for k in range(8):
                    P.stt("dve", ntmp[k % 2], xres[:, k, bs], Amod[:, i, k:k + 1], rstd2[blk], ALU.mult, ALU.mult)
                    P.act(dst[:, k, bs], ntmp[k % 2], AF.Identity, bias=mod[:, 3 * i * 8 + k:3 * i * 8 + k + 1], scale=1.0)

        cnt = {"g": 0, "d": 0, "x": 0, "c": 0}

        def ffn(fi):
            wg_v = wg_d[fi].rearrange("(k p) f -> p k f", p=128)
            wu_v = wu_d[fi].rearrange("(k p) f -> p k f", p=128)
            wd_v = wd_d[fi].rearrange("(f p) d -> p f d", p=128)
            i = 2 * fi
            norm_mod(i, hT, have_sums=(fi == 1))
            for pr in range(NF // 2):
                slot = wgu[pr % 2]
                P.dma("pool", slot[:, 0], wg_v[:, :, pr * 256:(pr + 1) * 256])
                P.dma("pool", slot[:, 1], wu_v[:, :, pr * 256:(pr + 1) * 256])
                for sub in range(2):
                    j = pr * 2 + sub
                    for blk in range(NBLK):
                        bs = slice(blk * 512, (blk + 1) * 512)
                        x = cnt["g"] % 2
                        cnt["g"] += 1
                        pg, pu = psf[x], psf[2 + x]
                        for k in range(8):
                            P.mm(pg, slot[:, 0, k, sub * 128:(sub + 1) * 128], hT[:, k, bs], start=(k == 0), stop=(k == 7))
                        for k in range(8):
                            P.mm(pu, slot[:, 1, k, sub * 128:(sub + 1) * 128], hT[:, k, bs], start=(k == 0), stop=(k == 7))
                        P.act(sgt[x], pg, AF.Silu)
                        P.tt("dve", actb[:, j, bs], sgt[x], pu, ALU.mult)
                        bg_step(2 if (pr * 4 + sub * 2 + blk) < 4 else 1)
            if bg["gen"] is not None:
                for _ in range(8):
                    if bg["j"] < 24:
                        bg_step(1)
            for pr in range(4):
                slot = wdn[pr % 2]
                P.dma("pool", slot, wd_v[:, :, pr * 256:(pr + 1) * 256])
                for sub in range(2):
                    dtile = pr * 2 + sub
                    for blk in range(NBLK):
                        bs = slice(blk * 512, (blk + 1) * 512)
                        pd = psf[4 + cnt["d"] % 2]
                        cnt["d"] += 1
                        for f in range(NF):
                            P.mm(pd, slot[:, f, sub * 128:(sub + 1) * 128], actb[:, f, bs], start=(f == 0), stop=(f == NF - 1))
                        P.stt("dve", xres[:, dtile, bs], pd, gate[:, i, dtile:dtile + 1], xres[:, dtile, bs], ALU.mult, ALU.add)
                        res_sumsq(dtile, blk)
                        bg_step(1)
            res_sumsq_flush()
            bg_step(100)

        def nextps4():
            x = cnt["x"] % 4
            cnt["x"] += 1
            return x

        def mixer(pi, a1=False, last_pre=False):
            norm_mod(1, h2T, have_sums=True)
            P.copy("dve", xbc_raw[:, :, 0:3], xhalo)
            P.copy("dve", u_tok[:, 0, :], uhalo)
            def dt_block():
                for c in range(NCH):
                    cs = slice(c * 128, (c + 1) * 128)
                    for k in range(8):
                        P.mm(psf[4][:, c * 16:(c + 1) * 16], h2T[:, k, cs], wdt[:, k, :], start=(k == 0), stop=(k == 7))
                NH = NCH * 16
                P.tt("dve", sp0, psf[4][:, 0:NH], dtb8, ALU.add)
                P.ts("dve", sp1, sp0, -1.0, None, ALU.mult)
                P.tt("dve", sp1, sp1, sp0, ALU.min)
                P.act(sp1, sp1, AF.Exp)
                P.act(sp1, sp1, AF.Ln, bias=onec[:, 0:1], scale=1.0)
                P.ts("dve", sp0, sp0, 0.0, None, ALU.max)
                P.tt("dve", dtall, sp0, sp1, ALU.add)
                P.tt("dve", adt, dtall, aneg8, ALU.mult)
                P.mm(psf[5][:, 0:NH], tri_f, adt)
                P.mm(psf[5][:, NH:2 * NH], ones_f, adt)
                P.copy("dve", acs, psf[5][:, 0:NH])
                P.ts("dve", nacs, acs, -1.0, None, ALU.mult)
                P.act(sd, acs, AF.Exp)
                P.tt("dve", sp0, psf[5][:, NH:2 * NH], acs, ALU.subtract)
                P.act(sp0, sp0, AF.Exp)
                P.tt("dve", dtds, dtall, sp0, ALU.mult)
                P.act(cdb, psf[5][:, NH:2 * NH], AF.Exp)
                P.copy("dve", acs_hb, acs)
                P.copy("dve", sp1, acs_hb)
                P.tt("dve", sp0, acs, sp1, ALU.subtract)

            for pr in range(8):
                if pr == 3:
                    dt_block()
                if a1 and pr >= 6 and not last_pre:
                    continue
                slot = wst[pr % 2]
                P.dma("pool", slot[:, :, 0:256], w_in_v[:, :, 1024 + pr * 256:1024 + (pr + 1) * 256])
                for sub in range(2):
                    i = pr * 2 + sub
                    for blk in range(NBLK):
                        if a1 and pr >= 6 and blk != NBLK - 1:
                            continue
                        bs = slice(blk * 512, (blk + 1) * 512)
                        pp = psf[nextps4()]
                        for k in range(8):
                            P.mm(pp, slot[:, k, sub * 128:(sub + 1) * 128], h2T[:, k, bs], start=(k == 0), stop=(k == 7))
                        P.copy(alt(), xbc_raw[:, i, 3 + blk * 512:3 + (blk + 1) * 512], pp)
            for grp in range(4):
                if a1 and (grp < 2 or not last_pre):
                    continue
                slot = wst[grp % 2]
                col0 = (0, 512, 3088, 3600)[grp]
                P.dma("pool", slot, w_in_v[:, :, col0:col0 + 512])
                for c in (range(NCH - 1, NCH) if a1 else range(NCH)):
                    cs = slice(c * 128, (c + 1) * 128)
                    pp = psf[nextps4()]
                    for k in range(8):
                        P.mm(pp, h2T[:, k, cs], slot[:, k, :], start=(k == 0), stop=(k == 7))
                    if grp < 2:
                        P.act(z_tok[:, c, grp * 512:(grp + 1) * 512], pp, AF.Silu)
                    else:
                        P.copy(alt(), u_tok[:, c + 1, (grp - 2) * 512:(grp - 1) * 512], pp)
            for i in range(12 if a1 else 16):
                if i % 3 == 1:
                    for blk in range(NBLK):
                        acc = ntmp[blk % 2]
                        P.ts("dve", acc, xbc_raw[:, i, blk * 512:blk * 512 + 512], cw[:, i, 0:1], None, ALU.mult)
                        for k in range(1, 4):
                            P.stt("dve", acc, xbc_raw[:, i, blk * 512 + k:blk * 512 + k + 512], cw[:, i, k:k + 1], acc, ALU.mult, ALU.add)
                        P.act(xbc_c[:, i, blk * 512:(blk + 1) * 512], acc, AF.Silu, bias=cb[:, i:i + 1], scale=1.0)
                    continue
                dg = diag4[cnt["c"] % 2]
                cnt["c"] += 1
                for k in range(4):
                    P.ts("dve", dg[:, k, :], ident_f, cw[:, i, k:k + 1], None, ALU.mult)
                for blk in range(NBLK):
                    pp = psf[nextps4()]
                    for k in range(4):
                        P.mm(pp, dg[:, k, :], xbc_raw[:, i, blk * 512 + k:blk * 512 + k + 512], start=(k == 0), stop=(k == 3))
                    P.act(xbc_c[:, i, blk * 512:(blk + 1) * 512], pp, AF.Silu, bias=cb[:, i:i + 1], scale=1.0)
            P.copy("dve", xhalo, xbc_raw[:, :, TP:TP + 3])
            for blk in range(0 if a1 else NBLK):
                bs = slice(blk * 512, (blk + 1) * 512)
                for i in range(8):
                    g = i // 2
                    pp = psf[i % 2]
                    for cc in range(4):
                        c = blk * 4 + cc
                        first = (pi == 0 and c == 0)
                        Mx = poolM[:, g, :] if first else poolM[:, 4 + g, :]
                        P.mm(pp[:, cc * 128:(cc + 1) * 128], u_tok[:, c + 1, i * 128:(i + 1) * 128], Mx, start=True, stop=False)
                        P.mm(pp[:, cc * 128:(cc + 1) * 128], u_tok[:, c, i * 128:(i + 1) * 128], poolM[:, 8 + g, :], start=False, stop=True)
                    P.copy(alt(), diffT[:, i, :], pp)
                for ot in range(8):
                    g, kd = ot // 2, ot % 2
                    pp = psf[2 + ot % 2]
                    for kc in range(2):
                        P.mm(pp, poolw[:, g, kc, kd * 128:(kd + 1) * 128], diffT[:, g * 2 + kc, :], start=(kc == 0), stop=(kc == 1))
                    P.act(ycatT[:, 8 + ot, bs], pp, AF.Identity, bias=pbs[:, ot:ot + 1], scale=psc[:, ot:ot + 1])
            if (not a1) or last_pre:
                P.copy("dve", uhalo, u_tok[:, NCH, :])
            def a1_front(c):
                cs = slice(c * 128, (c + 1) * 128)
                hs = slice(c * 16, (c + 1) * 16)
                par = c % 2
                tb0, tb1 = (0, 1) if par == 0 else (2, 4)
                for i in range(8):
                    P.transpose(psb[tb0][:, i * 128:(i + 1) * 128], xbc_c[:, i, cs], ident_b)
                for i in range(4):
                    P.transpose(psb[tb1][:, i * 128:(i + 1) * 128], xbc_c[:, 8 + i, cs], ident_b)
                P.tt("dve", xdtds[par].rearrange("p (h q) -> p h q", h=16), psb[tb0].rearrange("p (h q) -> p h q", h=16),
                     bc_last(dtds[:, hs], 64), ALU.mult)
                P.copy("act", Btok[par], psb[tb1][:, 0:512])

            def a1_back(c):
                hs = slice(c * 16, (c + 1) * 16)
                par = c % 2
                sb0, sb1 = (7, 3) if par == 0 else (5, 6)
                for g in range(4):
                    bank = sb0 if g < 2 else sb1
                    P.mm(psf[bank][:, (g % 2) * 256:(g % 2 + 1) * 256], Btok[par][:, g * 128:(g + 1) * 128],
                         xdtds[par][:, g * 256:(g + 1) * 256])
                P.tt("dve", prev.rearrange("p (h q) -> p h q", h=16), prev.rearrange("p (h q) -> p h q", h=16),
                     bc_last(cdb[:, hs], 64), ALU.mult)
                for b2 in range(2):
                    bank = sb0 if b2 == 0 else sb1
                    P.tt("dve", prev[:, b2 * 512:(b2 + 1) * 512], prev[:, b2 * 512:(b2 + 1) * 512], psf[bank], ALU.add)

            if a1:
                a1_front(0)
                for c in range(NCH):
                    if c + 1 < NCH:
                        a1_front(c + 1)
                    a1_back(c)
            def front(c):
                q = c % 2
                cs = slice(c * 128, (c + 1) * 128)
                hs = slice(c * 16, (c + 1) * 16)
                for i in range(8):
                    P.transpose(psb[0][:, i * 128:(i + 1) * 128], xbc_c[:, i, cs], ident_b)
                for i in range(4):
                    P.transpose(psb[1][:, i * 128:(i + 1) * 128], xbc_c[:, 8 + i, cs], ident_b)
                cb4 = psf[2].rearrange("p (g l) -> p g l", g=4)
                for g in range(4):
                    P.mm(cb4[:, g, :], xbc_c[:, 8 + g, cs], xbc_c[:, 12 + g, cs])
                P.tt("pool", Rhi, bc_mid(ident_f, 16), bc_last(sp1[:, hs], 128), ALU.mult)
                P.tt("pool", Rlo, bc_mid(ident_f, 16), bc_last(sp0[:, hs], 128), ALU.mult)
                yield
                xs3 = psb[0].rearrange("p (h q) -> p h q", h=16)
                P.tt("dve", xdt[q].rearrange("p (h q) -> p h q", h=16), xs3, bc_last(dtall[:, hs], 64), ALU.mult)
                P.tt("dve", xsD[q].rearrange("p (h q) -> p h q", h=16), xs3, bc_last(headp[:, 32:48], 64), ALU.mult)
                P.tt("dve", xdtds[q].rearrange("p (h q) -> p h q", h=16), xs3, bc_last(dtds[:, hs], 64), ALU.mult)
                P.copy("act", Btok[q], psb[1][:, 0:512])
                yield
                for g in range(4):
                    if g == 1:
                        P.tt("pool", prev.rearrange("p (h q) -> p h q", h=16), prev.rearrange("p (h q) -> p h q", h=16),
                             bc_last(cdb[:, hs], 64), ALU.mult)
                        for b2 in range(2):
                            for g_ in (2 * b2, 2 * b2 + 1):
                                P.mm(psf[7][:, (g_ % 2) * 256:(g_ % 2 + 1) * 256], Btok[q][:, g_ * 128:(g_ + 1) * 128],
                                     xdtds[q][:, g_ * 256:(g_ + 1) * 256])
                            P.tt("dve", prev[:, b2 * 512:(b2 + 1) * 512], prev[:, b2 * 512:(b2 + 1) * 512], psf[7], ALU.add)
                        P.copy("act", prev_b2[(c + 1) % 2], prev)
                        yield
                    sp_ = psf[g % 2]
                    P.mm(sp_, ones_b, Rhi[:, g * 4:(g + 1) * 4, :].rearrange("p a b -> p (a b)"), start=True, stop=False)
                    P.mm(sp_, ones_b, Rlo[:, g * 4:(g + 1) * 4, :].rearrange("p a b -> p (a b)"), start=False, stop=False)
                    P.mm(sp_, ident_b, mask_b, start=False, stop=True)
                    sp3 = sp_.rearrange("p (a b) -> p a b", a=4)
                    for r in range(4):
                        h = g * 4 + r
                        P.act(decT[q][:, h, :], sp3[:, r, :], AF.Exp, bias=nacs[:, c * 16 + h:c * 16 + h + 1], scale=1.0)
                    P.tt("dve", scT[q][:, g * 4:(g + 1) * 4, :], decT[q][:, g * 4:(g + 1) * 4, :], bc_mid(cb4[:, g, :], 4), ALU.mult)
                    yield

            def back(c):
                q = c % 2
                cs = slice(c * 128, (c + 1) * 128)
                hs = slice(c * 16, (c + 1) * 16)
                for g in range(4):
                    P.mm(psf[3 + g // 2][:, (g % 2) * 256:(g % 2 + 1) * 256], xbc_c[:, 12 + g, cs], prev_b2[c % 2][:, g * 256:(g + 1) * 256])
                for b2 in range(2):
                    P.mm(psf[5 + b2], ident_b, xsD[q][:, b2 * 512:(b2 + 1) * 512], start=True, stop=False)
                    for hh in range(8):
                        h = b2 * 8 + hh
                        P.mm(psf[5 + b2][:, hh * 64:(hh + 1) * 64], scT[q][:, h, :], xdt[q][:, h * 64:(h + 1) * 64], start=False, stop=(hh == 7))
                yield
                for b2 in range(2):
                    t13 = t1[:, b2 * 512:(b2 + 1) * 512].rearrange("p (h q) -> p h q", h=8)
                    P.tt("dve", t13, psf[3 + b2].rearrange("p (h q) -> p h q", h=8),
                         bc_last(sd[:, c * 16 + b2 * 8:c * 16 + b2 * 8 + 8], 64), ALU.mult)
                yield
                for b2 in range(2):
                    P.tt("dve", t1[:, b2 * 512:(b2 + 1) * 512], t1[:, b2 * 512:(b2 + 1) * 512], psf[5 + b2], ALU.add)
                yield
                P.tt("dve", t1, t1, z_tok[:, c, :], ALU.mult)
                for g in range(4):
                    gs = slice(g * 256, (g + 1) * 256)
                    P.stt_acc("dve", szb[:, gs], t1[:, gs], 1.0, t1[:, gs], ALU.mult, ALU.mult, ssq[:, g:g + 1])
                yield
                P.act(grs, ssq, AF.Ln, bias=epsc[:, 0:1], scale=1.0 / 256)
                P.act(grs, grs, AF.Exp, scale=-0.5)
                for g in range(4):
                    gs = slice(g * 256, (g + 1) * 256)
                    P.stt("dve", ynb[:, gs], t1[:, gs], grs[:, g:g + 1], snw[:, gs], ALU.mult, ALU.mult)
                yield
                for i in range(8):
                    P.transpose(psb[3][:, i * 128:(i + 1) * 128], ynb[:, i * 128:(i + 1) * 128], ident_b)
                P.copy("act", ycatT[:, 0:8, cs], psb[3].rearrange("p (a b) -> p a b", a=8))
                yield

            def drain(*gens):
                gens = list(gens)
                while gens:
                    for g_ in list(gens):
                        try:
                            next(g_)
                        except StopIteration:
                            gens.remove(g_)

            if not a1:
                P.copy("act", prev_b2[0], prev)
                drain(front(0))
                for c in range(NCH):
                    if c + 1 < NCH:
                        drain(front(c + 1), back(c))
                    else:
                        drain(back(c))
            for pr in range(0 if a1 else 4):
                slot = wo[pr % 2]
                P.dma("pool", slot, w_out_v[:, :, pr * 256:(pr + 1) * 256])
                for sub in range(2):
                    dtile = pr * 2 + sub
                    for blk in range(NBLK):
                        bs = slice(blk * 512, (blk + 1) * 512)
                        pp = psf[4 + cnt["d"] % 2]
                        cnt["d"] += 1
                        for k in range(16):
                            P.mm(pp, slot[:, k, sub * 128:(sub + 1) * 128], ycatT[:, k, bs], start=(k == 0), stop=(k == 15))
                        P.stt("dve", xres[:, dtile, bs], pp, gate[:, 1, dtile:dtile + 1], xres[:, dtile, bs], ALU.mult, ALU.add)
                        res_sumsq(dtile, blk)
            res_sumsq_flush()

        def final(pi):
            for blk in range(NBLK):
                bs = slice(blk * 512, (blk + 1) * 512)
                sumsq_rstd(blk, True)
                for k in range(8):
                    P.stt("dve", ostage2[blk][:, k, :], xres[:, k, bs], nw[:, 3, k:k + 1], rstd2[blk], ALU.mult, ALU.mult)
                    if pi + 1 < n_pass:
                        P.dma("sp", xres[:, k, bs], xT_v[:, k, (pi + 1) * TP + blk * 512:(pi + 1) * TP + (blk + 1) * 512])
                P.dma("sp", outT_v[:, :, pi * TP + blk * 512:pi * TP + (blk + 1) * 512], ostage2[blk], is_output=True)

        def dump(i):
            if debug:
                P.dma("sp", dbg[i].rearrange("(k p) t -> p k t", p=128), xres, is_output=True)

        for pa in range(n_pre):
            for blk in range(NBLK):
                P.dma("sp", xres[:, :, blk * 512:(blk + 1) * 512], xTa_v[:, :, pa * TP + blk * 512:pa * TP + (blk + 1) * 512])
            ffn(0)
            mixer(1, a1=True, last_pre=(pa == n_pre - 1))
        if n_pre:
            P.ts("dve", prev, prev, role[:, 0:1], None, ALU.mult)
            P.copy("act", prev_b, prev)
            P.ts("dve", xhalo, xhalo, role[:, 0:1], None, ALU.mult)
            P.ts("dve", uhalo, uhalo, role[:, 0:1], None, ALU.mult)
        for pi in range(n_pass):
            if pi == 0:
                for blk in range(NBLK):
                    P.dma("sp", xres[:, :, blk * 512:(blk + 1) * 512], xT_v[:, :, blk * 512:(blk + 1) * 512])
            ffn(0)
            if pi == 0:
                dump(0)
            mixer(pi)
            if pi == 0:
                dump(1)
            ffn(1)
            if pi == 0:
                dump(2)
            final(pi)
        P.emit(st)
    return nc, P.stats


def _consts(first_is_seq_start):
    ident = np.eye(128, dtype=np.float32)
    tri = np.triu(np.ones((128, 128), np.float32))
    ones = np.ones((128, 128), np.float32)
    constf = np.concatenate([ident, tri, ones], axis=1)
    s = np.arange(128)[:, None]
    l = np.arange(128)[None, :]
    maskT = np.where(l >= s, 0.0, NEG).astype(np.float32)
    maskT = np.tile(maskT, (1, 4))
    mats = []
    windows = (2, 4, 8, 16)
    t = np.arange(128)[None, :]
    for kind in ("first", "diag", "off"):
        for w in windows:
            if kind == "off":
                sp_ = np.arange(128)[:, None] - 128
                m = np.where((t - sp_ >= 0) & (t - sp_ <= w - 1), 1.0 / w, 0.0)
            else:
                if kind == "first" and first_is_seq_start:
                    cntv = np.minimum(t + 1, w).astype(np.float64)
                else:
                    cntv = np.full_like(t, w, dtype=np.float64)
                m = np.where((t - s >= 0) & (t - s <= w - 1), 1.0 / cntv, 0.0) - (s == t)
            mats.append(m.astype(np.float32))
    poolM = np.concatenate(mats, axis=1)
    return constf, maskT, poolM


_CACHE = {}


def _prep_inputs(b, t0, n_tok, seq_start, inp, n_pre_tok=0):
    f = np.float32
    c = np.ascontiguousarray
    constf, maskT, poolM = _consts(seq_start)
    nwv = np.stack([inp["ffn1_norm"][0], inp["mix_norm"][0], inp["ffn2_norm"][0], inp["final_norm"]])
    m = {
        "xT": c(inp["x"][b, t0:t0 + n_tok].T),
        "xTa": c(inp["x"][b, 0:t0].T) if t0 > 0 else np.zeros((D, max(n_pre_tok, TP)), f),
        "role": np.full((128, 1), 1.0 if t0 > 0 else 0.0, f),
        "c_bc": c(np.broadcast_to(inp["c"][b], (128, D))),
        "w_adaT": _CACHE["w_adaT"],
        "b_ada": c(inp["b_ada"][0].reshape(72, 128).T),
        "nw": c(nwv.reshape(4, 8, 128).transpose(2, 0, 1).reshape(128, 32)),
        "wg1": _CACHE["wg1"], "wu1": _CACHE["wu1"], "wd1": _CACHE["wd1"],
        "wg2": _CACHE["wg2"], "wu2": _CACHE["wu2"], "wd2": _CACHE["wd2"],
        "w_in": _CACHE["w_in"],
        "conv_w": c(inp["conv_w"][0].reshape(4, 16, 128).transpose(2, 1, 0).reshape(128, 64)),
        "conv_b": c(inp["conv_b"][0].reshape(16, 128).T),
        "headp": c(np.broadcast_to(np.concatenate([inp["dt_bias"][0], inp["a_log"][0], inp["d_skip"][0]]), (128, 48))),
        "ssd_norm_w": c(np.broadcast_to(inp["ssd_norm_w"][0], (128, D))),
        "pool_w": _CACHE["pool_w"],
        "pool_b": c(inp["pool_b"][0].reshape(8, 128).T),
        "pool_scale": c(inp["pool_scale"][0].reshape(8, 128).T),
        "w_out": _CACHE["w_out"],
        "constf": constf, "maskT": maskT, "poolM": poolM,
    }
    return {k: np.asarray(v, dtype=f) for k, v in m.items()}


def kernel(**inputs):
    inp = {k: np.asarray(v) for k, v in inputs.items()}
    c = np.ascontiguousarray
    _CACHE["w_adaT"] = c(inp["w_ada"][0].T.reshape(72, 128, D).transpose(1, 0, 2))
    for nm, key in (("wg1", "ffn1_w_gate"), ("wu1", "ffn1_w_up"), ("wd1", "ffn1_w_down"),
                    ("wg2", "ffn2_w_gate"), ("wu2", "ffn2_w_up"), ("wd2", "ffn2_w_down"),
                    ("w_in", "w_in"), ("pool_w", "pool_w"), ("w_out", "w_out")):
        _CACHE[nm] = c(inp[key][0])
    H = L // 2
    n_pass = H // TP
    nc, stats = build_program(n_pass, n_pre=n_pass)
    in_maps = [_prep_inputs(core // 2, (core % 2) * H, H, core % 2 == 0, inp, n_pre_tok=H) for core in range(8)]
    res = run_bass_kernel_spmd(nc, in_maps, core_ids=list(range(8)))
    out = np.empty((4, L, D), np.float32)
    for core in range(8):
        out[core // 2, (core % 2) * H:(core % 2 + 1) * H] = res.results[core]["outT"].T
    return out
```
